# Optimizing a Trainium2 kernel written in Bass

```python
import jax, jax.numpy as jnp
from jax import lax
import numpy as np

D_MODEL = 1024
BATCH = 8
SEQ = 2048
DEPTH = 1
DEC_BATCH = 128
DEC_SEQ = 1
PAST_LEN = 16384
PAGE_SIZE = 128

N_META = 16
MIX_WIDTH = D_MODEL
HG_WIDTH = MIX_WIDTH // 2
HG_HEADS = 4
HG_DK = HG_WIDTH // HG_HEADS
HG_DV = HG_WIDTH // HG_HEADS
CV_WIDTH = MIX_WIDTH - HG_WIDTH
CONV_K = 31
D_FF = 4 * D_MODEL
CHUNK = 64
EPS = 1e-6
IN_COLS = 4 * HG_WIDTH + 2 * CV_WIDTH
IN_SPLITS = (HG_WIDTH, 2 * HG_WIDTH, 3 * HG_WIDTH, 4 * HG_WIDTH, 4 * HG_WIDTH + CV_WIDTH)

kernel_name = 'hymba_hgrn2_conformer_decode_step'


def _rmsnorm(x, g):
    xf = x.astype(jnp.float32)
    y = xf * lax.rsqrt(jnp.mean(xf * xf, axis=-1, keepdims=True) + EPS)
    return (y * g.astype(jnp.float32)).astype(x.dtype)


def _layernorm(x, g, b):
    xf = x.astype(jnp.float32)
    xc = xf - jnp.mean(xf, axis=-1, keepdims=True)
    y = xc * lax.rsqrt(jnp.mean(xc * xc, axis=-1, keepdims=True) + EPS)
    return (y * g.astype(jnp.float32) + b.astype(jnp.float32)).astype(x.dtype)


def _layer_lower_bound(lb_params, layer):
    p = jax.nn.softmax(lb_params.astype(jnp.float32), axis=0)
    return jnp.cumsum(p, axis=0)[layer]


def _hgrn2_chunked(q, k, v, logf, s0):
    bsz, t, h, _ = q.shape
    dv = v.shape[-1]
    n = t // CHUNK

    def blocks(a):
        return a.reshape(bsz, n, CHUNK, h, a.shape[-1]).transpose(0, 1, 3, 2, 4)

    q, k, v, logf = blocks(q), blocks(k), blocks(v), blocks(logf)
    b = jnp.cumsum(logf, axis=3)
    b_end = b[:, :, :, -1:, :]
    q_dec = q * jnp.exp(b)
    k_inv = k * jnp.exp(-b)
    k_end = k * jnp.exp(b_end - b)
    causal = jnp.tril(jnp.ones((CHUNK, CHUNK), dtype=bool))
    scores = jnp.where(causal, jnp.einsum('bnhtd,bnhsd->bnhts', q_dec, k_inv), 0.0)
    o_intra = jnp.einsum('bnhts,bnhsv->bnhtv', scores, v)
    ds = jnp.einsum('bnhsd,bnhsv->nbhdv', k_end, v)
    decay = jnp.exp(b_end[:, :, :, 0, :]).transpose(1, 0, 2, 3)

    def step(s, inp):
        dec, d = inp
        return dec[..., None] * s + d, s

    s_fin, s_start = lax.scan(step, s0, (decay, ds))
    o_inter = jnp.einsum('bnhtd,nbhdv->bnhtv', q_dec, s_start)
    o = (o_intra + o_inter).transpose(0, 1, 3, 2, 4).reshape(bsz, t, h, dv)
    return o, s_fin


def _hgrn2_recurrent(q, k, v, logf, s0):
    def step(s, inp):
        qt, kt, vt, lft = inp
        s = jnp.exp(lft)[..., None] * s + kt[..., :, None] * vt[..., None, :]
        return s, jnp.einsum('bhd,bhdv->bhv', qt, s)

    xs = tuple(a.transpose(1, 0, 2, 3) for a in (q, k, v, logf))
    s_fin, o = lax.scan(step, s0, xs)
    return o.transpose(1, 0, 2, 3), s_fin


def _layer(h, s0, buf, chunked, lb, norm1_g, w_in, hg_onorm_g, conv_w, conv_b, conv_ln_g, conv_ln_b,
           w_out, norm2_g, w_up, w_down):
    bsz, t, _ = h.shape
    f32 = jnp.float32
    hn = _rmsnorm(h, norm1_g)
    z = jnp.einsum('btd,dc->btc', hn, w_in)
    q_r, f_r, i_r, g_r, a_r, b_r = jnp.split(z, IN_SPLITS, axis=-1)

    f = lb + (1.0 - lb) * jax.nn.sigmoid(f_r.astype(f32))
    heads = lambda a: a.reshape(bsz, t, HG_HEADS, a.shape[-1] // HG_HEADS)
    q = heads(jax.nn.silu(q_r.astype(f32)))
    k = heads(1.0 - f)
    logf = heads(jnp.log(f))
    v = heads(i_r.astype(f32))
    s0 = s0.astype(f32)
    if chunked:
        front = (-N_META) % CHUNK
        back = (-(front + t)) % CHUNK
        pad = lambda a: jnp.pad(a, ((0, 0), (front, back), (0, 0), (0, 0)))
        o, s_new = _hgrn2_chunked(pad(q), pad(k), pad(v), pad(logf), s0)
        o = o[:, front:front + t]
    else:
        o, s_new = _hgrn2_recurrent(q, k, v, logf, s0)
    gate = heads(jax.nn.silu(g_r.astype(f32)))
    y_hg = (_rmsnorm(o, hg_onorm_g) * gate).reshape(bsz, t, HG_WIDTH).astype(h.dtype)

    glu = a_r * jax.nn.sigmoid(b_r)
    xcat = jnp.concatenate([buf.astype(glu.dtype), glu], axis=1)
    new_buf = xcat[:, -(CONV_K - 1):]
    dw = lax.conv_general_dilated(xcat, conv_w[:, None, :].astype(xcat.dtype), window_strides=(1,),
                                  padding='VALID', dimension_numbers=('NWC', 'WIO', 'NWC'),
                                  feature_group_count=CV_WIDTH) + conv_b
    y_cv = jax.nn.silu(_layernorm(dw, conv_ln_g, conv_ln_b)).astype(h.dtype)

    h = h + jnp.concatenate([y_hg, y_cv], axis=-1) @ w_out
    hn2 = _rmsnorm(h, norm2_g)
    h = h + jnp.square(jax.nn.relu(hn2 @ w_up)) @ w_down
    return h, s_new, new_buf


def setup_inputs(seed: int = 0) -> dict:
    key = jax.random.key(seed)
    ks = jax.random.split(key, 20)
    nrm = lambda k, shape, s: jax.random.normal(k, shape, jnp.float32) * s
    return {
        'x_prompt': nrm(ks[0], (BATCH, SEQ, D_MODEL), 1.0),
        'x_sample': nrm(ks[1], (DEC_BATCH, DEC_SEQ, D_MODEL), 1.0),
        'state_hgrn': nrm(ks[2], (DEPTH, DEC_BATCH, HG_HEADS, HG_DK, HG_DV), 0.3),
        'state_conv': nrm(ks[3], (DEPTH, DEC_BATCH, CONV_K - 1, CV_WIDTH), 0.5),
        'meta_tokens': nrm(ks[4], (N_META, D_MODEL), 1.0),
        'hg_lb': nrm(ks[5], (DEPTH + 1, HG_WIDTH), 0.1),
        'norm1_g': 1.0 + nrm(ks[6], (DEPTH, D_MODEL), 0.02),
        'w_in': nrm(ks[7], (DEPTH, D_MODEL, IN_COLS), D_MODEL ** -0.5),
        'hg_onorm_g': 1.0 + nrm(ks[8], (DEPTH, HG_DV), 0.02),
        'conv_w': nrm(ks[9], (DEPTH, CONV_K, CV_WIDTH), CONV_K ** -0.5),
        'conv_b': nrm(ks[10], (DEPTH, CV_WIDTH), 0.02),
        'conv_ln_g': 1.0 + nrm(ks[11], (DEPTH, CV_WIDTH), 0.02),
        'conv_ln_b': nrm(ks[12], (DEPTH, CV_WIDTH), 0.02),
        'w_out': nrm(ks[13], (DEPTH, MIX_WIDTH, D_MODEL), MIX_WIDTH ** -0.5),
        'norm2_g': 1.0 + nrm(ks[14], (DEPTH, D_MODEL), 0.02),
        'w_up': nrm(ks[15], (DEPTH, D_MODEL, D_FF), D_MODEL ** -0.5),
        'w_down': nrm(ks[16], (DEPTH, D_FF, D_MODEL), D_FF ** -0.5),
        'final_g': 1.0 + nrm(ks[17], (D_MODEL,), 0.02),
    }


def reference(x_prompt, x_sample, state_hgrn, state_conv, meta_tokens, hg_lb, norm1_g, w_in, hg_onorm_g,
              conv_w, conv_b, conv_ln_g, conv_ln_b, w_out, norm2_g, w_up, w_down, final_g):
    bp = x_prompt.shape[0]
    meta = jnp.broadcast_to(meta_tokens[None].astype(x_prompt.dtype), (bp, N_META, D_MODEL))
    hp = jnp.concatenate([meta, x_prompt], axis=1)
    hs = x_sample
    sp_list, cp_list, ss_list, cs_list = [], [], [], []
    for l in range(DEPTH):
        lb = _layer_lower_bound(hg_lb, l)
        w = (norm1_g[l], w_in[l], hg_onorm_g[l], conv_w[l], conv_b[l], conv_ln_g[l], conv_ln_b[l],
             w_out[l], norm2_g[l], w_up[l], w_down[l])
        s0_p = jnp.zeros((bp, HG_HEADS, HG_DK, HG_DV), jnp.float32)
        buf_p = jnp.zeros((bp, CONV_K - 1, CV_WIDTH), x_prompt.dtype)
        hp, s_p, c_p = _layer(hp, s0_p, buf_p, True, lb, *w)
        hs, s_s, c_s = _layer(hs, state_hgrn[l], state_conv[l], False, lb, *w)
        sp_list.append(s_p)
        cp_list.append(c_p)
        ss_list.append(s_s)
        cs_list.append(c_s)
    y_prompt = _rmsnorm(hp[:, N_META:], final_g)
    y_sample = _rmsnorm(hs, final_g)
    new_state_hgrn_prompt = jnp.stack(sp_list).astype(state_hgrn.dtype)
    new_state_conv_prompt = jnp.stack(cp_list).astype(state_conv.dtype)
    new_state_hgrn_sample = jnp.stack(ss_list).astype(state_hgrn.dtype)
    new_state_conv_sample = jnp.stack(cs_list).astype(state_conv.dtype)
    return (y_prompt, y_sample, new_state_hgrn_prompt, new_state_conv_prompt, new_state_hgrn_sample, new_state_conv_sample)
```

```python
import numpy as np
from contextlib import ExitStack
import concourse.bass as bass
import concourse.mybir as mybir
from concourse.bass_utils import run_bass_kernel_spmd

F32 = mybir.dt.float32
BF16 = mybir.dt.bfloat16
AF = mybir.ActivationFunctionType
ALU = mybir.AluOpType
EPS = 1e-6
NCORES = 8
SEQ = 2048
NSMP = 16
ENGS = ("pe", "act", "dve", "pool", "sp")
CAST_ENG = ("pool", "act", "dve", "pool", "act", "pool", "act", "dve")


class Prog:
    def __init__(self, nc, stack):
        self.nc = nc
        self.stack = stack
        self.ops = {e: [] for e in ENGS}
        self.cnt = {e: 0 for e in ENGS}
        self.sem = {e: stack.enter_context(nc.semaphore("sem_" + e)) for e in ENGS}
        self.semname = {id(self.sem[e]): e for e in ENGS}
        self.waited = {e: {} for e in ENGS}
        self.last_w = {}
        self.readers = {}
        self.dma_sems = {}
        self.dma_cnt = {}
        self.out_dma = []

    def dsem(self, name):
        if name not in self.dma_sems:
            self.dma_sems[name] = self.stack.enter_context(self.nc.semaphore("dq_" + name))
            self.dma_cnt[name] = 0
        return self.dma_sems[name]

    def op(self, eng, fn, reads=(), writes=(), signal=True, dma=None, is_out=False):
        deps = {}

        def add(h):
            s, v = h
            k = id(s)
            if k not in deps or deps[k][1] < v:
                deps[k] = (s, v)

        for k in reads:
            if k in self.last_w:
                add(self.last_w[k])
        for k in writes:
            if k in self.last_w:
                add(self.last_w[k])
            for h in self.readers.get(k, {}).values():
                add(h)
        waits = []
        wd = self.waited[eng]
        for k, (s, v) in deps.items():
            if eng == "pe" and s is self.sem["pe"]:
                continue
            if wd.get(k, 0) < v:
                wd[k] = v
                waits.append((s, v))
        if dma is not None:
            s = self.dsem(dma)
            self.dma_cnt[dma] += 16
            h = (s, self.dma_cnt[dma])
            inc = (s, 16)
            if is_out:
                self.out_dma.append(dma)
        elif signal:
            self.cnt[eng] += 1
            h = (self.sem[eng], self.cnt[eng])
            inc = (self.sem[eng], 1)
        else:
            h = (self.sem[eng], self.cnt[eng] + 1)
            inc = None
        for k in reads:
            r = self.readers.setdefault(k, {})
            kk = id(h[0])
            if kk not in r or r[kk][1] < h[1]:
                r[kk] = h
        for k in writes:
            self.last_w[k] = h
            self.readers[k] = {}
        self.ops[eng].append((waits, fn, inc))

    def replay(self, eng, e):
        for waits, fn, inc in self.ops[eng]:
            for s, v in waits:
                e.wait_ge(s, v)
            ins = fn(e)
            if inc is not None:
                ins.then_inc(inc[0], inc[1])
        if eng == "sp":
            for name in dict.fromkeys(self.out_dma):
                e.wait_ge(self.dma_sems[name], self.dma_cnt[name])


def build_program(nseg=4, with_sample=True, dbg_stop=None):
    nc = bass.Bass("TRN2", target_bir_lowering=False, dynamic_dma_scratch_size=2048)
    T = nseg * 512

    def din(name, shape):
        return nc.dram_tensor(name, shape, F32, kind="ExternalInput").ap()

    def dout(name, shape):
        return nc.dram_tensor(name, shape, F32, kind="ExternalOutput").ap()

    xp = din("xp", [T, 1024])
    xsd = din("xs", [NSMP, 1024])
    shd = din("sh", [NSMP, 4, 128, 128])
    scd = din("sc", [NSMP, 30, 512])
    metad = din("meta", [16, 1024])
    w_in = din("w_in", [1024, 3072])
    w_out = din("w_out", [1024, 1024])
    w_up = din("w_up", [1024, 4096])
    w_down = din("w_down", [4096, 1024])
    rowsd = din("rows", [37, 128])
    cwd = din("conv_w", [31, 512])
    fgd = din("final_g", [1, 1024])
    identd = din("ident", [128, 128])
    mask2d = din("mask2", [128, 128])
    mscand = din("mscan", [128, 128])
    yp = dout("yp", [T, 1024])
    ysd = dout("ys", [NSMP, 1024])
    shp = dout("shp", [4, 128, 128])
    scp = dout("scp", [30, 512])
    shs = dout("shs", [NSMP, 4, 128, 128])
    scs = dout("scs", [NSMP, 30, 512])
    wcache = nc.dram_tensor("wcache", [12, 128, 8192], BF16, kind="Internal").ap()

    st = ExitStack()
    with st:
        P = Prog(nc, st)

        def sb(name, shape, dt=F32):
            return st.enter_context(nc.sbuf_tensor(name, shape, dt))

        def ps(name, shape, dt=F32):
            return st.enter_context(nc.psum_tensor(name, shape, dt))

        h = sb("h", [128, 5, 1024])
        hn2T = sb("hn2T", [128, 8, 528], BF16)
        arena = sb("arena", [128, 4, 8, 1024], BF16)
        stg = sb("stg", [128, 2, 1024])
        diag = sb("diag", [128, 4, 31, 128], BF16)
        ident_b = sb("ident_b", [128, 128], BF16)
        ident_f = sb("ident_f", [128, 128])
        mask2 = sb("mask2s", [128, 128])
        mscan = sb("mscans", [128, 128])
        ones_f = sb("ones_f", [128, 128])
        rows = sb("rowss", [37, 128])
        cols = sb("cols", [128, 40])
        dc = sb("dcols", [128, 32])
        cwrows = sb("cwrows", [31, 512])
        cwT = sb("cwT", [128, 4, 32])
        fgrow = sb("fgrow", [1, 1024])
        fg_bc = sb("fg_bc", [128, 1024])
        S = sb("S", [128, 4, 128])
        S_bf = sb("S_bf", [128, 2, 4, 128], BF16)
        xs_b = sb("xs_b", [128, 2, 1024], BF16)
        hnT = sb("hnT", [128, 2, 8, 128], BF16)
        r_th2 = sb("r_th2", [128, 4, 128])
        r_sq = sb("r_sq", [128, 4, 128])
        LF = sb("LF", [128, 4, 128])
        BB = sb("BB", [128, 4, 128])
        EN = sb("EN", [128, 4, 128])
        q_dec = sb("q_dec", [128, 2, 4, 128], BF16)
        k_inv = sb("k_inv", [128, 2, 4, 128], BF16)
        k_endT = sb("k_endT", [128, 4, 128], BF16)
        k_end = sb("k_end", [128, 2, 512], BF16)
        v_t = sb("v_t", [128, 2, 512], BF16)
        ovlB = sb("ovlB", [128, 1024])
        r_th3 = sb("r_th3", [128, 2, 128])
        dec = sb("dec", [128, 2, 4, 2])
        scm = sb("scm", [128, 4, 128], BF16)
        y_hg = sb("y_hg", [128, 2, 512], BF16)
        stat = sb("stat", [128, 64])
        convbuf = sb("convbuf", [128, 4, 542], BF16)
        mixT = sb("mixT", [128, 8, 528], BF16)
        ovlA = sb("ovlA", [128, 4096])
        glu_last = sb("glu_last", [128, 4, 32])
        gl_out = sb("gl_out", [32, 512])
        hnTs = sb("hnTs", [128, 8, 16], BF16)
        uTs = sb("uTs", [128, 2, 8, 16], BF16)
        sm = sb("sm", [128, 14, 64])
        print('SBUF bytes remaining', nc.sbuf_bytes_remaining)
        ptr = ps("ptr", [128, 2, 1024], BF16)
        pmm = ps("pmm", [128, 6, 512])

        gate = ovlB[:].rearrange("p (r f) -> p r f", r=2)
        ln_t = ovlB[:].rearrange("p (a f) -> p a f", a=4)
        rbuf = gate
        dw = ovlA[:, 0:2048].rearrange("p (c f) -> p c f", c=4)
        dwsq = ovlA[:, 2048:3072].rearrange("p (r f) -> p r f", r=2)
        t1r = ovlA[:, 3072:4096].rearrange("p (r f) -> p r f", r=4)
        uT = [ovlA[:, i * 2048:(i + 1) * 2048].bitcast(BF16).rearrange("p (c f) -> p c f", c=8) for i in range(2)]

        ring = {}

        def nxt(name, n):
            ring[name] = (ring.get(name, -1) + 1) % n
            return ring[name]

        pm_state = {"banks": [0, 1, 2, 3, 4, 5], "i": 0}

        def pm_alloc():
            b = pm_state["banks"][pm_state["i"] % len(pm_state["banks"])]
            pm_state["i"] += 1
            return b, "pmm%d" % b

        def pm_pair():
            assert len(pm_state["banks"]) == 6
            while pm_state["banks"][pm_state["i"] % 6] % 2 == 1:
                pm_state["i"] += 1
            b = pm_state["banks"][pm_state["i"] % 6]
            pm_state["i"] += 2
            return b, ["pmm%d" % b, "pmm%d" % (b + 1)]

        def ptr_alloc():
            i = nxt("ptr", 2)
            return i, "ptr%d" % i

        def stat_alloc(w=1):
            i = nxt("stat", 16)
            return stat[:, i * 4:i * 4 + w], "stat%d" % i

        def dma(out, in_, reads, writes, ring_name, is_out=False, eng="sp"):
            P.op(eng, lambda e, o=out, i=in_: e.dma_start(out=o, in_=i), reads=reads, writes=writes,
                 dma=ring_name, is_out=is_out)

        dma(ident_f[:], identd, [], ["ident_f"], "c0")
        dma(mask2[:], mask2d, [], ["mask2"], "c1")
        dma(mscan[:], mscand, [], ["mscan"], "c2")
        dma(rows[:], rowsd, [], ["rows"], "c3")
        dma(cwrows[:], cwd, [], ["cwrows"], "c4")
        dma(fgrow[:], fgd, [], ["fgrow"], "c5")
        P.op("dve", lambda e: e.tensor_copy(out=ident_b[:], in_=ident_f[:]), ["ident_f"], ["ident_b"])
        P.op("dve", lambda e: e.memset(ones_f[:], 1.0), [], ["ones_f"])
        P.op("dve", lambda e: e.memset(S[:], 0.0), [], ["S0", "S1", "S2", "S3"])
        P.op("dve", lambda e: e.memset(S_bf[:], 0.0), [], ["Sbf%d_%d" % (c, i) for c in range(4) for i in range(2)])
        P.op("dve", lambda e: e.memset(convbuf[:], 0.0), [], ["cb%d" % c for c in range(4)])
        bi, bk = pm_alloc()
        P.op("pe", lambda e, b=bi: e.transpose(out=pmm[:, b, 0:37], in_=rows[:], identity=ident_f[0:37, 0:37]),
             ["rows", "ident_f"], [bk])
        P.op("dve", lambda e, b=bi: e.tensor_copy(out=cols[:, 0:37], in_=pmm[:, b, 0:37]), [bk], ["cols"])
        P.op("dve", lambda e: e.memset(cols[:, 37:38], 1.0), [], ["cols1"])
        ONE = cols[:, 37:38]
        for c in range(4):
            bi, bk = pm_alloc()
            P.op("pe", lambda e, b=bi, c=c: e.transpose(out=pmm[:, b, 0:31], in_=cwrows[:, c * 128:(c + 1) * 128],
                                                         identity=ident_f[0:31, 0:31]), ["cwrows", "ident_f"], [bk])
            P.op("dve", lambda e, b=bi, c=c: e.tensor_copy(out=cwT[:, c, 0:31], in_=pmm[:, b, 0:31]), [bk], ["cwT"])
        for n in range(2):
            bi, bk = pm_alloc()
            P.op("pe", lambda e, b=bi, n=n: e.matmul(pmm[:, b, :], lhsT=ones_f[0:1, :], rhs=fgrow[0:1, n * 512:(n + 1) * 512],
                                                     start=True, stop=True), ["ones_f", "fgrow"], [bk])
            P.op("dve", lambda e, b=bi, n=n: e.tensor_copy(out=fg_bc[:, n * 512:(n + 1) * 512], in_=pmm[:, b, :]), [bk], ["fg_bc"])
        P.op("dve", lambda e: e.tensor_tensor(out=dc[:, 24:28], in0=cols[:, 0:4], in1=cols[:, 4:8], op=ALU.subtract), ["cols"], ["dc_d"])
        P.op("act", lambda e: e.activation(out=dc[:, 0:4], in_=dc[:, 24:28], func=AF.Tanh, scale=0.5), ["dc_d"], ["dc_t"])
        P.op("dve", lambda e: e.tensor_scalar(out=dc[:, 4:8], in0=dc[:, 0:4], scalar1=-0.5, scalar2=0.5, op0=ALU.mult, op1=ALU.add), ["dc_t"], ["dc_a"])
        P.op("dve", lambda e: e.tensor_scalar(out=dc[:, 8:12], in0=dc[:, 0:4], scalar1=0.25, scalar2=-0.25, op0=ALU.mult, op1=ALU.add), ["dc_t"], ["dc_b"])
        P.op("dve", lambda e: e.tensor_scalar(out=dc[:, 12:16], in0=dc[:, 0:4], scalar1=0.25, scalar2=0.75, op0=ALU.mult, op1=ALU.add), ["dc_t"], ["dc_c"])
        P.op("dve", lambda e: e.tensor_scalar(out=dc[:, 16:20], in0=dc[:, 0:4], scalar1=-0.25, scalar2=0.25, op0=ALU.mult, op1=ALU.add), ["dc_t"], ["dc_e"])
        P.op("act", lambda e: e.activation(out=dc[:, 20:24], in_=dc[:, 16:20], func=AF.Ln), ["dc_e"], ["dc_f"])
        DCK = ["dc_a", "dc_b", "dc_c", "dc_e", "dc_f"]
        C1 = lambda c: dc[:, 8 + c:9 + c]
        C2 = lambda c: dc[:, 12 + c:13 + c]
        HOML = lambda c: dc[:, 16 + c:17 + c]
        LNHOML = lambda c: dc[:, 20 + c:21 + c]
        def build_diag():
          for c in range(4):
            for j in range(31):
                P.op("dve", lambda e, c=c, j=j: e.tensor_scalar(out=diag[:, c, j, :], in0=ident_b[:], scalar1=cwT[:, c, j:j + 1],
                                                                scalar2=0.5, op0=ALU.mult, op1=ALU.mult),
                     ["ident_b", "cwT"], ["diag"])

        slot_ring = [0]

        preconv_done = set()

        def preconv_chunks():
            out = []
            for q, which in ((3, 0), (3, 1)):
                if True:
                    wid = 4 + 2 * q + which
                    preconv_done.add(wid)
                    for k in range(8):
                        if which == 0:
                            src = w_up[k * 128:(k + 1) * 128, q * 1024:(q + 1) * 1024]
                            sc = cols[:, 16 + k:17 + k]
                        else:
                            src = w_down[q * 1024 + k * 128:q * 1024 + (k + 1) * 128, :]
                            sc = ONE

                        def f(wid=wid, k=k, src=src, sc=sc):
                            i = nxt("stg", 2)
                            bi = nxt("bnc", 4)
                            bap = hn2T[:, 2 * bi:2 * bi + 2, 0:512]
                            dma(stg[:, i, :], src, [], ["stg%d" % i], "pstg%d" % i, eng="pool")
                            P.op("pool", lambda e: e.tensor_scalar(out=bap, in0=stg[:, i, :].rearrange("p (a f) -> p a f", a=2), scalar1=sc, scalar2=1.0,
                                                                   op0=ALU.mult, op1=ALU.mult), ["stg%d" % i, "cols", "cols1"], ["bnc%d" % bi])
                            dma(wcache[wid][:, k * 1024:(k + 1) * 1024].rearrange("p (a f) -> p a f", a=2), bap, ["bnc%d" % bi], ["wc%dk%d" % (wid, k)], "bncq%d" % bi, eng="pool")
                        out.append(f)
            return out

        def load_slot(src_fn, scale_fn, wid, seg, src2_fn=None):
            s = slot_ring[0] % 4
            slot_ring[0] += 1
            skeys = ["slot%dk%d" % (s, k) for k in range(8)]
            if seg > 0 or wid in preconv_done:
                dma(arena[:, s, :, :], wcache[wid].rearrange("p (k c) -> p k c", k=8), ["wc%d" % wid] + ["wc%dk%d" % (wid, k) for k in range(8)], skeys, "slotld%d" % s)
                return s
            if slot_ring[0] <= 4 and src2_fn is not None:
                for kk in range(4):
                    i = nxt("stgbig", 3)
                    if i == 0:
                        sap, gkeys, sring = stg[:, :, :], ["stg0", "stg1"], "big0"
                    else:
                        sap, gkeys, sring = h[:, 2 * i - 2:2 * i, :], ["h%d" % (2 * i - 2), "h%d" % (2 * i - 1)], "big%d" % i
                    dma(sap, src2_fn(kk), [], gkeys, sring)
                    for hf in range(2):
                        k = 2 * kk + hf
                        ce = CAST_ENG[k]
                        sc = scale_fn(k)
                        if ce == "act":
                            P.op("act", lambda e, s=s, k=k, sap=sap, sc=sc, hf=hf: e.activation(out=arena[:, s, k, :], in_=sap[:, hf, :], func=AF.Copy, scale=sc),
                                 gkeys + ["cols", "cols1"], ["slot%dk%d" % (s, k)])
                        else:
                            P.op(ce, lambda e, s=s, k=k, sap=sap, sc=sc, hf=hf: e.tensor_scalar(out=arena[:, s, k, :], in0=sap[:, hf, :], scalar1=sc, scalar2=1.0,
                                                                                    op0=ALU.mult, op1=ALU.mult),
                                 gkeys + ["cols", "cols1"], ["slot%dk%d" % (s, k)])
                if nseg > 1:
                    dma(wcache[wid].rearrange("p (k c) -> p k c", k=8), arena[:, s, :, :], skeys, ["wc%d" % wid], "wcw%d" % wid)
                return s
            for k in range(8):
                if slot_ring[0] <= 4:
                    i = nxt("stg7", 7)
                    if i < 2:
                        sap, gkeys, sring = stg[:, i, :], ["stg%d" % i], "stg%d" % i
                    else:
                        sap, gkeys, sring = h[:, i - 2, :], ["h%d" % (i - 2)], "xin%d" % (i - 2)
                else:
                    i = nxt("stg3", 3) if wid < 8 else nxt("stg4", 4)
                    if i < 2:
                        sap, gkeys, sring = stg[:, i, :], ["stg%d" % i], "stg%d" % i
                    elif i == 2:
                        sap, gkeys, sring = hnT[:].rearrange("p r k t -> p (r k t)").bitcast(F32), ["hnT0", "hnT1"], "stgx3"
                    else:
                        sap, gkeys, sring = xs_b[:].rearrange("p r f -> p (r f)").bitcast(F32), ["xs0", "xs1"], "stgx2"
                dma(sap, src_fn(k), [], gkeys, sring)
                ce = CAST_ENG[k] if slot_ring[0] <= 4 else "pool"
                sc = scale_fn(k)
                if ce == "act":
                    P.op("act", lambda e, s=s, k=k, sap=sap, sc=sc: e.activation(out=arena[:, s, k, :], in_=sap, func=AF.Copy, scale=sc),
                         gkeys + ["cols", "cols1"], ["slot%dk%d" % (s, k)])
                else:
                    P.op(ce, lambda e, s=s, k=k, sap=sap, sc=sc: e.tensor_scalar(out=arena[:, s, k, :], in0=sap, scalar1=sc, scalar2=1.0,
                                                                             op0=ALU.mult, op1=ALU.mult),
                         gkeys + ["cols", "cols1"], ["slot%dk%d" % (s, k)])
            if nseg > 1:
                dma(wcache[wid].rearrange("p (k c) -> p k c", k=8), arena[:, s, :, :], skeys, ["wc%d" % wid], "wcw%d" % wid)
            return s

        def load_win(part, seg):
            return load_slot(lambda k: w_in[k * 128:(k + 1) * 128, part * 1024:(part + 1) * 1024], lambda k: cols[:, 8 + k:9 + k], part, seg,
                             lambda kk: w_in[kk * 256:(kk + 1) * 256, part * 1024:(part + 1) * 1024].rearrange("(k p) c -> p k c", p=128))

        def load_wout_pool():
            s = slot_ring[0] % 4
            slot_ring[0] += 1
            skeys = ["slot%dk%d" % (s, k) for k in range(8)]
            for k in range(8):
                i = nxt("stg", 2)
                sc = cols[:, 24:25] if k < 4 else ONE
                dma(stg[:, i, :], w_out[k * 128:(k + 1) * 128, :], [], ["stg%d" % i], "pstg%d" % i, eng="pool")
                P.op("pool", lambda e, k=k, i=i, sc=sc: e.tensor_scalar(out=arena[:, s, k, :], in0=stg[:, i, :], scalar1=sc, scalar2=1.0, op0=ALU.mult, op1=ALU.mult),
                     ["stg%d" % i, "cols", "cols1"], ["slot%dk%d" % (s, k)])
            if nseg > 1:
                dma(wcache[3].rearrange("p (k c) -> p k c", k=8), arena[:, s, :, :], skeys, ["wc3"], "wcw3")
            return s

        def load_wout(seg):
            if seg == 0 and nseg > 1 and dbg_stop is None:
                return load_wout_pool()
            return load_slot(lambda k: w_out[k * 128:(k + 1) * 128, :], lambda k: (cols[:, 24:25] if k < 4 else ONE), 3, seg)

        def load_wup(q, seg):
            return load_slot(lambda k: w_up[k * 128:(k + 1) * 128, q * 1024:(q + 1) * 1024], lambda k: cols[:, 16 + k:17 + k], 4 + 2 * q, seg)

        def load_wdown(q, seg):
            return load_slot(lambda k: w_down[q * 1024 + k * 128:q * 1024 + (k + 1) * 128, :], lambda k: ONE, 5 + 2 * q, seg)

        def rstd_from_ss(ss_ap, ss_key, n, inv_n):
            o, ok = stat_alloc(ss_ap.shape[1])
            P.op("act", lambda e: e.activation(out=o[0:n, :], in_=ss_ap, func=AF.Ln, scale=inv_n, bias=EPS_AP[0:n, :]), [ss_key, "cols1"], [ok])
            P.op("act", lambda e: e.activation(out=o[0:n, :], in_=o[0:n, :], func=AF.Exp, scale=-0.5), [ok], [ok])
            return o, ok

        P.op("dve", lambda e: e.memset(cols[:, 38:39], EPS), [], ["cols1"])
        EPS_AP = cols[:, 38:39]

        def front_a(x_ap, x_key, n):
            xi = nxt("xs", 2)
            ss, ssk = stat_alloc()
            P.op("act", lambda e: e.activation(out=xs_b[0:n, xi, :], in_=x_ap, func=AF.Square, accum_out=ss[0:n, :]),
                 [x_key], ["xs%d" % xi, ssk])
            rs, rsk = rstd_from_ss(ss[0:n, :], ssk, n, 1.0 / 1024)
            P.op("dve", lambda e: e.tensor_scalar(out=xs_b[0:n, xi, :], in0=x_ap, scalar1=rs[0:n, :], scalar2=None, op0=ALU.mult),
                 [x_key, rsk], ["xs%d" % xi])
            return xi

        def front_b(xi, n, dst_fn, dst_key):
            pi, pk = ptr_alloc()
            pv = ptr[:, pi, :].rearrange("p (k t) -> p k t", k=8)
            for k in range(8):
                P.op("pe", lambda e, k=k: e.transpose(out=pv[:, k, 0:n], in_=xs_b[0:n, xi, k * 128:(k + 1) * 128], identity=ident_b[0:n, 0:n]),
                     ["xs%d" % xi, "ident_b"], [pk], signal=(k == 7))
            P.op("act", lambda e: e.activation(out=dst_fn(), in_=pv[:, :, 0:n], func=AF.Copy), [pk], [dst_key])

        def front(x_ap, x_key, n, dst_fn, dst_key):
            front_b(front_a(x_ap, x_key, n), n, dst_fn, dst_key)

        def proj_fm(slot, co, c, rhs_fn, n, rkeys):
            bi, bk = pm_alloc()
            for k in range(8):
                P.op("pe", lambda e, k=k, b=bi: e.matmul(pmm[:, b, 0:n], lhsT=arena[:, slot, k, co + c * 128:co + (c + 1) * 128], rhs=rhs_fn(k),
                                                         start=(k == 0), stop=(k == 7)),
                     ["slot%dk%d" % (slot, k)] + rkeys, [bk], signal=(k == 7))
            return bi, bk

        def proj_tm(slot, co, lhs_fn, n, rkeys):
            bi, bk = pm_alloc()
            for k in range(8):
                P.op("pe", lambda e, k=k, b=bi: e.matmul(pmm[0:n, b, :], lhsT=lhs_fn(k), rhs=arena[:, slot, k, co:co + 512],
                                                         start=(k == 0), stop=(k == 7)),
                     ["slot%dk%d" % (slot, k)] + rkeys, [bk], signal=(k == 7))
            return bi, bk

        aux = {"eng": "dve"}

        def f_chain(bi, bk, c, n, ti, nch):
            P.op("act", lambda e: e.activation(out=r_th2[:, c, 0:n], in_=pmm[:, bi, 0:n], func=AF.Tanh, scale=-0.5), [bk], ["th2_%d" % c])
            P.op(aux["eng"], lambda e: e.tensor_scalar(out=LF[:, c, 0:n], in0=r_th2[:, c, 0:n], scalar1=C1(c), scalar2=C2(c), op0=ALU.mult, op1=ALU.add),
                 ["th2_%d" % c] + DCK, ["lf"])
            P.op(aux["eng"], lambda e: e.tensor_scalar(out=r_th2[:, c, 0:n], in0=r_th2[:, c, 0:n], scalar1=HOML(c), scalar2=HOML(c), op0=ALU.mult, op1=ALU.add),
                 ["th2_%d" % c] + DCK, ["th2_%d" % c])

        def g_wide(n, ti, nch, with_q):
            g_wide_a(n)
            g_wide_b(n, ti, nch, with_q)

        def g_wide_a(n):
            P.op("act", lambda e: e.activation(out=LF[:, :, 0:n], in_=LF[:, :, 0:n], func=AF.Ln), ["lf"], ["lf"])
            for c in range(4):
                P.op("dve", lambda e, c=c: e.tensor_tensor_scan(out=BB[:, c, 0:n], data0=mscan[:, 0:n], data1=LF[:, c, 0:n], initial=0.0,
                                                                op0=ALU.mult, op1=ALU.add), ["lf", "mscan"], ["bb"])

        def g_wide_b(n, ti, nch, with_q):
            H4 = [0, 1, 2, 3]
            P.op("act", lambda e: e.activation(out=EN[:, :, 0:n], in_=BB[:, :, 0:n], func=AF.Exp, scale=-1.0), ["bb"], ["en"])
            P.op("act", lambda e: e.activation(out=BB[:, :, 0:n], in_=BB[:, :, 0:n], func=AF.Exp), ["bb"], ["bb"])
            P.op("dve", lambda e: e.tensor_tensor(out=k_inv[:, ti, :, 0:n], in0=r_th2[:, :, 0:n], in1=EN[:, :, 0:n], op=ALU.mult),
                 ["th2_%d" % c for c in H4] + ["en"], ["kinv%d_%d" % (ti, c) for c in H4])
            if with_q:
                P.op("dve", lambda e: e.tensor_tensor(out=q_dec[:, ti, :, 0:n], in0=r_sq[:, :, 0:n], in1=BB[:, :, 0:n], op=ALU.mult),
                     ["sq%d" % c for c in H4] + ["bb"], ["qdec%d_%d" % (ti, c) for c in H4])
            cs = n // nch
            P.op("dve", lambda e: e.tensor_copy(out=dec[:, ti, :, 0:nch], in_=BB[:, :, cs - 1:n:cs]), ["bb"], ["dec%d_%d" % (ti, c) for c in H4])
            if nch == 2:
                kin = k_inv[:, ti, :, :].rearrange("p c (h s) -> p (c h) s", s=cs)
                kout = k_endT[:, :, :].rearrange("p c (h s) -> p (c h) s", s=cs)
                dbc = dec[:, ti, :, :].rearrange("p c (h o) -> p (c h) o", o=1).to_broadcast([128, 8, cs])
            else:
                kin = k_inv[:, ti, :, 0:n]
                kout = k_endT[:, :, 0:n]
                dbc = dec[:, ti, :, 0:1].to_broadcast([128, 4, n])
            P.op("dve", lambda e: e.tensor_tensor(out=kout, in0=kin, in1=dbc, op=ALU.mult),
                 ["kinv%d_%d" % (ti, c) for c in H4] + ["dec%d_%d" % (ti, c) for c in H4], ["kendT%d" % c for c in H4])

        def g_transposes(n, ti):
            H4 = [0, 1, 2, 3]
            pi, pk = ptr_alloc()
            for c in range(4):
                P.op("pe", lambda e, c=c: e.transpose(out=ptr[0:n, pi, c * 128:(c + 1) * 128], in_=k_endT[:, c, 0:n], identity=ident_b[:]), ["kendT%d" % c, "ident_b"], [pk],
                     signal=(c == 3))
            P.op("dve", lambda e: e.tensor_copy(out=k_end[0:n, ti, :], in_=ptr[0:n, pi, 0:512]), [pk], ["kend%d_%d" % (ti, c) for c in H4])

        def state_update(c, ch, ti, dSb, dSk, slot_cur):
            P.op("dve", lambda e: e.scalar_tensor_tensor(out=S[:, c, :], in0=S[:, c, :], scalar=dec[:, ti, c, ch:ch + 1], in1=pmm[:, dSb, c * 128:(c + 1) * 128],
                                                         op0=ALU.mult, op1=ALU.add), ["S%d" % c, "dec%d_%d" % (ti, c), dSk], ["S%d" % c])
            nx = 1 - slot_cur
            P.op(aux["eng"], lambda e: e.tensor_copy(out=S_bf[:, nx, c, :], in_=S[:, c, :]), ["S%d" % c], ["Sbf%d_%d" % (c, nx)])

        sbf_cur = [0, 0, 0, 0]

        def meta_prologue(sQF, sIG, sAB):
            xm = h[0:16, 4, :]
            dma(xm, metad, [], ["h4"], "xin4")
            front(xm, "h4", 16, lambda: hnT[:, 0, :, 0:16], "hnT0")
            nxt("hnT", 2)
            rhs = lambda k: hnT[:, 0, k, 0:16]
            vb, vk = proj_tm(sIG, 0, lambda k: hnT[:, 0, k, 0:16], 16, ["hnT0"])
            P.op("act", lambda e: e.activation(out=v_t[0:16, 0, :], in_=pmm[0:16, vb, :], func=AF.Copy), [vk], ["v0"])
            for c in range(4):
                fb, fk = proj_fm(sQF, 512, c, rhs, 16, ["hnT0"])
                f_chain(fb, fk, c, 16, 0, 1)
            g_wide(16, 0, 1, False)
            g_transposes(16, 0)
            db, dk = pm_alloc()
            for c in range(4):
                P.op("pe", lambda e, c=c: e.matmul(pmm[:, db, c * 128:(c + 1) * 128], lhsT=k_end[0:16, 0, c * 128:(c + 1) * 128],
                                                   rhs=v_t[0:16, 0, c * 128:(c + 1) * 128], start=True, stop=True),
                     ["kend0_%d" % c, "v0"], [dk], signal=(c == 3))
            for c in range(4):
                state_update(c, 0, 0, db, dk, sbf_cur[c])
                sbf_cur[c] = 1 - sbf_cur[c]
            for c in range(4):
                bb, bbk = proj_fm(sAB, 512, c, rhs, 16, ["hnT0"])
                r = nxt("th3", 2)
                P.op("act", lambda e, bb=bb, r=r: e.activation(out=r_th3[:, r, 0:16], in_=pmm[:, bb, 0:16], func=AF.Tanh, scale=0.5), [bbk], ["th3_%d" % r])
                ab, abk = proj_fm(sAB, 0, c, rhs, 16, ["hnT0"])
                P.op("dve", lambda e, ab=ab, r=r, c=c: e.scalar_tensor_tensor(out=convbuf[:, c, 14:30], in0=r_th3[:, r, 0:16], scalar=1.0, in1=pmm[:, ab, 0:16],
                                                                             op0=ALU.add, op1=ALU.mult), ["th3_%d" % r, abk], ["cb%d" % c])

        def tile_parts(seg, j, slots, last_tile):
            t0 = seg * 512 + j * 128
            hk = "h%d" % j
            cx = {}

            def A0a():
                if "xi" in cx:
                    return
                dma(h[:, j, :], xp[t0:t0 + 128, :], [], [hk], "xin%d" % j)
                cx["xi"] = front_a(h[:, j, :], hk, 128)

            def A0b():
                if "hi" in cx:
                    return
                cx["hi"] = nxt("hnT", 2)
                cx["ti"] = nxt("tile", 2)
                hi = cx["hi"]
                front_b(cx["xi"], 128, lambda: hnT[:, hi, :, :], "hnT%d" % hi)

            def rhs(k):
                return hnT[:, cx["hi"], k, :]

            def A1():
                for c in range(4):
                    fb, fk = proj_fm(slots["QF"], 512, c, rhs, 128, ["hnT%d" % cx["hi"]])
                    f_chain(fb, fk, c, 128, cx["ti"], 2)

            def A2():
                for c in range(4):
                    qb, qk = proj_fm(slots["QF"], 0, c, rhs, 128, ["hnT%d" % cx["hi"]])
                    P.op("act", lambda e, qb=qb, c=c: e.activation(out=r_sq[:, c, :], in_=pmm[:, qb, 0:128], func=AF.Silu), [qk], ["sq%d" % c])

            def gAa():
                g_wide_a(128)

            def gAb():
                g_wide_b(128, cx["ti"], 2, True)

            def gB():
                g_transposes(128, cx["ti"])

            def A3():
                hkx = "hnT%d" % cx["hi"]
                for c in range(4):
                    bb, bbk = proj_fm(slots["AB"], 512, c, rhs, 128, [hkx])
                    r = nxt("th3", 2)
                    P.op("act", lambda e, bb=bb, r=r: e.activation(out=r_th3[:, r, :], in_=pmm[:, bb, 0:128], func=AF.Tanh, scale=0.5), [bbk], ["th3_%d" % r])
                    ab, abk = proj_fm(slots["AB"], 0, c, rhs, 128, [hkx])
                    P.op("dve", lambda e, ab=ab, r=r, c=c: e.scalar_tensor_tensor(out=convbuf[:, c, 30 + j * 128:30 + (j + 1) * 128], in0=r_th3[:, r, :], scalar=1.0,
                                                                                 in1=pmm[:, ab, 0:128], op0=ALU.add, op1=ALU.mult), ["th3_%d" % r, abk], ["cb%d" % c])
                    if last_tile:
                        P.op("dve", lambda e, ab=ab, r=r, c=c: e.scalar_tensor_tensor(out=glu_last[:, c, 0:30], in0=r_th3[:, r, 98:128], scalar=1.0,
                                                                                     in1=pmm[:, ab, 98:128], op0=ALU.add, op1=ALU.mult), ["th3_%d" % r, abk], ["glu_last"])

            def A4():
                hi, ti = cx["hi"], cx["ti"]
                hkx = "hnT%d" % hi
                vb, vk = proj_tm(slots["IG"], 0, lambda k: hnT[:, hi, k, :], 128, [hkx])
                P.op("dve", lambda e: e.tensor_copy(out=v_t[:, ti, :], in_=pmm[:, vb, :]), [vk], ["v%d" % ti])
                gb, gk = proj_tm(slots["IG"], 512, lambda k: hnT[:, hi, k, :], 128, [hkx])
                P.op("act", lambda e: e.activation(out=gate[:, ti, :], in_=pmm[:, gb, :], func=AF.Silu), [gk], ["ovlB%d" % ti])

            SB, OB = 0, 1
            sk_, ok_ = "pmm0", "pmm1"

            def B0():
                ti = cx["ti"]
                for c in range(4):
                    P.op("pe", lambda e, c=c: e.matmul(pmm[:, SB, c * 128:(c + 1) * 128], lhsT=k_inv[:, ti, c, :], rhs=q_dec[:, ti, c, :], start=True, stop=True),
                         ["kinv%d_%d" % (ti, c), "qdec%d_%d" % (ti, c)], [sk_], signal=(c == 3))
                for c in range(4):
                    P.op("dve", lambda e, c=c: e.tensor_tensor(out=scm[:, c, :], in0=pmm[:, SB, c * 128:(c + 1) * 128], in1=mask2[:], op=ALU.mult),
                         [sk_, "mask2"], ["scm%d" % c])

            def B1():
                ti = cx["ti"]
                for c in range(4):
                    P.op("pe", lambda e, c=c: e.matmul(pmm[:, OB, c * 128:(c + 1) * 128], lhsT=scm[:, c, :], rhs=v_t[:, ti, c * 128:(c + 1) * 128], start=(c == 0), stop=(c == 0),
                                                       skip_group_check=(c != 0)),
                         ["scm%d" % c, "v%d" % ti], [ok_], signal=False)

            def Bch(ch):
                ti = cx["ti"]
                lo, hi_ = ch * 64, (ch + 1) * 64
                for c in range(4):
                    cur = sbf_cur[c]
                    P.op("pe", lambda e, c=c, cur=cur: e.matmul(pmm[lo:hi_, OB, c * 128:(c + 1) * 128], lhsT=q_dec[:, ti, c, lo:hi_], rhs=S_bf[:, cur, c, :],
                                                                start=False, stop=True, skip_group_check=True),
                         ["qdec%d_%d" % (ti, c), "Sbf%d_%d" % (c, cur)], [ok_], signal=(ch == 1 and c == 3))
                for c in range(4):
                    P.op("pe", lambda e, c=c: e.matmul(pmm[:, SB, c * 128:(c + 1) * 128], lhsT=k_end[lo:hi_, ti, c * 128:(c + 1) * 128],
                                                       rhs=v_t[lo:hi_, ti, c * 128:(c + 1) * 128], start=True, stop=True),
                         ["kend%d_%d" % (ti, c), "v%d" % ti], [sk_], signal=(c == 3))
                for c in range(4):
                    state_update(c, ch, ti, SB, sk_, sbf_cur[c])
                    sbf_cur[c] = 1 - sbf_cur[c]

            def B4a():
                ti = cx["ti"]
                yi = nxt("yhg", 2)
                cx["yi"] = yi
                ss, ssk = stat_alloc(4)
                for c in range(4):
                    P.op("act", lambda e, c=c: e.activation(out=y_hg[:, yi, c * 128:(c + 1) * 128], in_=pmm[:, OB, c * 128:(c + 1) * 128], func=AF.Square,
                                                            accum_out=ss[:, c:c + 1]), [ok_], ["yhg%d" % yi, ssk])
                rs, rsk = rstd_from_ss(ss, ssk, 128, 1.0 / 128)
                for c in range(4):
                    P.op("dve", lambda e, c=c: e.scalar_tensor_tensor(out=y_hg[:, yi, c * 128:(c + 1) * 128], in0=pmm[:, OB, c * 128:(c + 1) * 128], scalar=rs[:, c:c + 1],
                                                                      in1=gate[:, ti, c * 128:(c + 1) * 128], op0=ALU.mult, op1=ALU.mult),
                         [ok_, rsk, "ovlB%d" % ti], ["yhg%d" % yi])

            def B4b():
                yi = cx["yi"]
                pi, pk = ptr_alloc()
                pv = ptr[:, pi, 0:512].rearrange("p (k t) -> p k t", k=4)
                for c in range(4):
                    P.op("pe", lambda e, c=c: e.transpose(out=pv[:, c, :], in_=y_hg[:, yi, c * 128:(c + 1) * 128], identity=ident_b[:]), ["yhg%d" % yi, "ident_b"], [pk], signal=(c == 3))
                P.op("act", lambda e: e.activation(out=mixT[:, 0:4, j * 128:(j + 1) * 128], in_=pv, func=AF.Copy), [pk], ["mixhg%d" % j])

            return dict(A0a=A0a, A0b=A0b, A1=A1, A2=A2, A3=A3, A4=A4, gAa=gAa, gAb=gAb, gB=gB, B0=B0, B1=B1, B2=lambda: Bch(0), B3=lambda: Bch(1), B4a=B4a, B4b=B4b)

        def run_step(B, A, F, prevB, filler, F3=None):
            fl = list(filler or [])

            def fill():
                if fl:
                    fl.pop(0)()
            if A:
                A["A1"]()
                A["A2"]()
            fill()
            if A: A["A3"]()
            if B: B["B0"]()
            if A: A["A4"]()
            fill()
            if B:
                B["B1"]()
                B["gB"]()
                B["B2"]()
            fill()
            if prevB: prevB["B4b"]()
            if F: F["A0b"]()
            fill()
            if B: B["B3"]()
            if A: A["gAa"]()
            if F3: F3["A0a"]()
            if A: A["gAb"]()
            if B: B["B4a"]()

        DWK = ["ovl0a", "ovl0b"]

        def conv_chunks():
            def mk(half, c):
                hs0 = half * 256

                def f():
                    bi, bk = pm_alloc()
                    for jt in range(31):
                        P.op("pe", lambda e, jt=jt, b=bi: e.matmul(pmm[:, b, 0:256], lhsT=diag[:, c, jt, :], rhs=convbuf[:, c, hs0 + jt:hs0 + jt + 256], start=(jt == 0), stop=(jt == 30)),
                             ["diag", "cb%d" % c], [bk], signal=(jt == 30))
                    P.op("act", lambda e, b=bi: e.activation(out=dw[:, c, hs0:hs0 + 256], in_=pmm[:, b, 0:256], func=AF.Identity, bias=cols[:, 25 + c:26 + c]), [bk, "cols"], [DWK[half]])
                    if half == 1:
                        P.op("dve", lambda e: e.tensor_copy(out=convbuf[:, c, 0:30], in_=convbuf[:, c, 512:542]), ["cb%d" % c], ["cb%d" % c])
                return f
            return [mk(hf, c) for hf in range(2) for c in range(4)]

        def ln_half(half):
            hs = slice(half * 256, (half + 1) * 256)
            dk = DWK[half]
            ab, ak = pm_alloc()
            for c in range(4):
                P.op("pe", lambda e, c=c, b=ab: e.matmul(pmm[:, b, 0:256], lhsT=ones_f[:], rhs=dw[:, c, hs], start=(c == 0), stop=(c == 3)), ["ones_f", dk], [ak], signal=(c == 3))
            qb_, qk_ = pm_alloc()
            for c in range(4):
                r = nxt("dwsq", 2)
                P.op("act", lambda e, c=c, r=r: e.activation(out=dwsq[:, r, 0:256], in_=dw[:, c, hs], func=AF.Square), [dk], ["ovl1"])
                P.op("pe", lambda e, c=c, r=r, b=qb_: e.matmul(pmm[:, b, 0:256], lhsT=ones_f[:], rhs=dwsq[:, r, 0:256], start=(c == 0), stop=(c == 3)), ["ones_f", "ovl1"], [qk_],
                     signal=True)
            mean, msq, rstd, nmr = ln_t[:, 0, :], ln_t[:, 1, :], ln_t[:, 2, :], ln_t[:, 3, :]
            P.op("dve", lambda e, b=ab: e.tensor_scalar(out=mean, in0=pmm[:, b, 0:256], scalar1=1.0 / 512, scalar2=None, op0=ALU.mult), [ak], ["ovlB0"])
            P.op("dve", lambda e: e.tensor_tensor(out=msq, in0=mean, in1=mean, op=ALU.mult), ["ovlB0"], ["ovlB0"])
            P.op("dve", lambda e, b=qb_: e.scalar_tensor_tensor(out=msq, in0=pmm[:, b, 0:256], scalar=1.0 / 512, in1=msq, op0=ALU.mult, op1=ALU.subtract), [qk_, "ovlB0"], ["ovlB0"])
            P.op("act", lambda e: e.activation(out=rstd, in_=msq, func=AF.Ln, bias=EPS_AP), ["ovlB0", "cols1"], ["ovlB1"])
            P.op("act", lambda e: e.activation(out=rstd, in_=rstd, func=AF.Exp, scale=-0.5), ["ovlB1"], ["ovlB1"])
            P.op("dve", lambda e: e.tensor_tensor(out=nmr, in0=mean, in1=rstd, op=ALU.mult), ["ovlB0", "ovlB1"], ["ovlB1"])
            rb = ln_t[:, 2:3, :].to_broadcast([128, 4, 256])
            nb_ = ln_t[:, 3:4, :].to_broadcast([128, 4, 256])
            P.op("dve", lambda e: e.tensor_tensor(out=dw[:, :, hs], in0=dw[:, :, hs], in1=rb, op=ALU.mult), [dk, "ovlB1"], [dk])
            P.op("dve", lambda e: e.tensor_tensor(out=dw[:, :, hs], in0=dw[:, :, hs], in1=nb_, op=ALU.subtract), [dk, "ovlB1"], [dk])
            for c in range(4):
                P.op("act", lambda e, c=c: e.activation(out=mixT[:, 4 + c, hs], in_=dw[:, c, hs], func=AF.Silu, scale=cols[:, 29 + c:30 + c], bias=cols[:, 33 + c:34 + c]),
                     [dk, "cols"], ["mixcv%d" % half])

        def seg_1bc(seg, sWO, last_seg, conv_rest, extra=None):
            ex = list(extra or [])

            def exrun():
                if ex:
                    ex.pop(0)()
            if last_seg:
                bi, bk = pm_alloc()
                for c in range(4):
                    P.op("pe", lambda e, c=c, b=bi: e.transpose(out=pmm[0:30, b, c * 128:(c + 1) * 128], in_=glu_last[:, c, 0:30], identity=ident_f[:]),
                         ["glu_last", "ident_f"], [bk], signal=(c == 3))
                P.op("act", lambda e, b=bi: e.activation(out=gl_out[0:30, :], in_=pmm[0:30, b, :], func=AF.Copy, scale=0.5), [bk], ["gl_out"])
                dma(scp, gl_out[0:30, :], ["gl_out"], [], "o_scp", is_out=True)
            n1 = min(4, len(conv_rest))
            for f in conv_rest[:len(conv_rest) - n1]:
                f()
            exrun()
            ln_half(0)
            exrun()
            for f in conv_rest[len(conv_rest) - n1:]:
                f()
            exrun()
            ln_half(1)
            pend = []
            for j in range(4):
                pend.append(wout_tile(j, 128, sWO, ["mixhg%d" % j, "mixcv%d" % (j // 2)], defer=True))
                if j >= 1:
                    pend.pop(0)()
            pend.pop(0)()
            while ex:
                ex.pop(0)()

        def wout_tile(j, n, sWO, mkeys, defer=False):
            cs = slice(j * 128, j * 128 + n)
            bi, bks = pm_pair()
            for nn in range(2):
                for kc in range(8):
                    P.op("pe", lambda e, nn=nn, kc=kc: e.matmul(pmm[0:n, bi + nn, :], lhsT=mixT[:, kc, cs], rhs=arena[:, sWO, kc, nn * 512:(nn + 1) * 512],
                                                             start=(kc == 0), stop=(kc == 7)),
                         mkeys + ["slot%dk%d" % (sWO, kc)], [bks[nn]], signal=(kc == 7))
            hk = "h%d" % j
            for nn in range(2):
                P.op("dve", lambda e, nn=nn: e.tensor_tensor(out=h[0:n, j, nn * 512:(nn + 1) * 512], in0=h[0:n, j, nn * 512:(nn + 1) * 512], in1=pmm[0:n, bi + nn, :], op=ALU.add),
                     [hk, bks[nn]], [hk])
            xi = front_a(h[0:n, j, :], hk, n)
            fb = lambda: front_b(xi, n, lambda: hn2T[:, :, cs], "hn2T%d" % j)
            if defer:
                return fb
            fb()

        def mlp_quarter(seg, q, sU, sD, ntile, nsmp, hooks=None, smp_ops=None):
            ncol = 512 + nsmp
            ui = nxt("uT", 2)
            uks = ["ovl0a", "ovl0b"] if ui == 0 else ["ovl1"]
            hkeys = ["hn2T%d" % j for j in range(ntile)]
            for fc in range(8):
                bi, bk = pm_alloc()
                for k in range(8):
                    P.op("pe", lambda e, k=k, fc=fc, b=bi: e.matmul(pmm[:, b, :], lhsT=arena[:, sU, k, fc * 128:(fc + 1) * 128], rhs=hn2T[:, k, 0:512], start=(k == 0), stop=(k == 7)),
                         ["slot%dk%d" % (sU, k)] + hkeys[0:4], [bk], signal=(k == 7))
                r = nxt("rbuf", 2)
                P.op("act", lambda e, b=bi, r=r: e.activation(out=rbuf[:, r, :], in_=pmm[:, b, :], func=AF.Relu), [bk], ["ovlB%d" % r])
                P.op("dve", lambda e, fc=fc, r=r: e.tensor_tensor(out=uT[ui][:, fc, :], in0=rbuf[:, r, :], in1=rbuf[:, r, :], op=ALU.mult), ["ovlB%d" % r], uks)
            if nsmp:
                usi = nxt("uTs", 2)
                bi, bk = pm_alloc()
                for fc in range(8):
                    for k in range(8):
                        P.op("pe", lambda e, k=k, fc=fc, bi=bi: e.matmul(pmm[:, bi, fc * 16:(fc + 1) * 16], lhsT=arena[:, sU, k, fc * 128:(fc + 1) * 128], rhs=hn2T[:, k, 512:528],
                                                                         start=(k == 0), stop=(k == 7)),
                             ["slot%dk%d" % (sU, k), "hn2T4"], [bk], signal=(fc == 7 and k == 7))
                P.op("act", lambda e, bi=bi: e.activation(out=sm[:, 12, :], in_=pmm[:, bi, 0:64], func=AF.Relu), [bk], ["s_rs"])
                P.op("act", lambda e, bi=bi: e.activation(out=sm[:, 13, :], in_=pmm[:, bi, 64:128], func=AF.Relu), [bk], ["s_tmp"])
                P.op("dve", lambda e: e.tensor_tensor(out=uTs[:, usi, 0:4, :], in0=v4(sm[:, 12, :]), in1=v4(sm[:, 12, :]), op=ALU.mult), ["s_rs"], ["uTs%d" % usi])
                P.op("dve", lambda e: e.tensor_tensor(out=uTs[:, usi, 4:8, :], in0=v4(sm[:, 13, :]), in1=v4(sm[:, 13, :]), op=ALU.mult), ["s_tmp"], ["uTs%d" % usi])
                bi, bks = pm_pair()
                for nn in range(2):
                    for fc in range(8):
                        P.op("pe", lambda e, nn=nn, fc=fc, bi=bi: e.matmul(pmm[0:16, bi + nn, :], lhsT=uTs[:, usi, fc, :], rhs=arena[:, sD, fc, nn * 512:(nn + 1) * 512],
                                                                          start=(fc == 0), stop=(fc == 7)),
                             ["uTs%d" % usi, "slot%dk%d" % (sD, fc)], [bks[nn]], signal=(fc == 7))
                for nn in range(2):
                    P.op("dve", lambda e, nn=nn, bi=bi: e.tensor_tensor(out=h[0:16, 4, nn * 512:(nn + 1) * 512], in0=h[0:16, 4, nn * 512:(nn + 1) * 512], in1=pmm[0:16, bi + nn, :], op=ALU.add),
                         ["h4", bks[nn]], ["h4"])
                if q == 3:
                    final_tile(h[0:16, 4, :], "h4", 16, ysd, "o_ys")
            if hooks and "after_up" in hooks:
                hooks["after_up"]()
            for j in range(4):
                bi, bks = pm_pair()
                for nn in range(2):
                    for fc in range(8):
                        P.op("pe", lambda e, nn=nn, fc=fc, j=j, bi=bi: e.matmul(pmm[:, bi + nn, :], lhsT=uT[ui][:, fc, j * 128:(j + 1) * 128], rhs=arena[:, sD, fc, nn * 512:(nn + 1) * 512],
                                                                      start=(fc == 0), stop=(fc == 7)),
                             uks + ["slot%dk%d" % (sD, fc)], [bks[nn]], signal=(fc == 7))
                hk = "h%d" % j
                for nn in range(2):
                    P.op("dve", lambda e, nn=nn, j=j, bi=bi: e.tensor_tensor(out=h[:, j, nn * 512:(nn + 1) * 512], in0=h[:, j, nn * 512:(nn + 1) * 512], in1=pmm[:, bi + nn, :], op=ALU.add),
                         [hk, bks[nn]], [hk])
                if q == 3:
                    final_tile(h[:, j, :], hk, 128, yp[seg * 512 + j * 128:seg * 512 + (j + 1) * 128, :], "o_yp%d" % j)
                    if hooks and j in hooks:
                        hooks[j]()
                if smp_ops:
                    smp_ops()

        def final_tile(hap, hk, n, dst, ringname):
            ss, ssk = stat_alloc()
            junk = y_hg[:].rearrange("p r f -> p (r f)")
            P.op("act", lambda e: e.activation(out=junk[0:n, :], in_=hap, func=AF.Square, accum_out=ss[0:n, :]), [hk], ["yhg0", "yhg1", ssk])
            rs, rsk = rstd_from_ss(ss[0:n, :], ssk, n, 1.0 / 1024)
            P.op("dve", lambda e: e.scalar_tensor_tensor(out=hap, in0=hap, scalar=rs[0:n, :], in1=fg_bc[0:n, :], op0=ALU.mult, op1=ALU.mult), [hk, rsk, "fg_bc"], [hk])
            dma(dst, hap, [hk], [], ringname, is_out=True)


        Sin = [ovlA[:, i * 512:(i + 1) * 512].rearrange("p (h v) -> p h v", h=4) for i in range(3)]
        Xc = [ovlA[0:30, 1536 + i * 512:1536 + (i + 1) * 512] for i in range(2)]
        vmk = ovlA[0:16, 2560:3072]
        k_tok = ovlA[0:16, 3072:3584]
        v_tok = ovlA[0:16, 3584:4096]
        glu_tok = gl_out[0:16, :]
        SMK = ["s_Sin0", "s_Sin1", "s_Sin2", "s_X0", "s_X1", "s_vm", "s_ktok", "s_vtok"]
        (QT, TH2, FT, KT, VT, GT, TH3, GLU, OT, DWT, DWS, SQ, RS, TMP) = [sm[:, i, :] for i in range(14)]
        v4 = lambda ap: ap.rearrange("p (c t) -> p c t", c=4)

        def sample_part1(sQF, sIG, sAB):
            dma(h[0:16, 4, :], xsd, [], ["h4"], "xin4")
            front(h[0:16, 4, :], "h4", 16, lambda: hnTs[:, :, :], "hnTs")
            zb, zk = pm_alloc()
            for g in range(24):
                slot = (sQF, sIG, sAB)[g // 8]
                co = (g % 8) * 128
                for k in range(8):
                    P.op("pe", lambda e, g=g, k=k, slot=slot, co=co: e.matmul(pmm[:, zb, g * 16:(g + 1) * 16], lhsT=arena[:, slot, k, co:co + 128], rhs=hnTs[:, k, :],
                                                                           start=(k == 0), stop=(k == 7)),
                         ["slot%dk%d" % (slot, k), "hnTs"], [zk], signal=(g == 23 and k == 7))
            z = lambda i: pmm[:, zb, i * 64:(i + 1) * 64]
            P.op("act", lambda e: e.activation(out=QT, in_=z(0), func=AF.Silu), [zk], ["s_qt"])
            P.op("act", lambda e: e.activation(out=GT, in_=z(3), func=AF.Silu), [zk], ["s_gt"])
            P.op("act", lambda e: e.activation(out=TH2, in_=z(1), func=AF.Tanh, scale=-0.5), [zk], ["s_th2"])
            P.op("act", lambda e: e.activation(out=TH3, in_=z(5), func=AF.Tanh, scale=0.5), [zk], ["s_th3"])
            P.op("act", lambda e: e.activation(out=VT, in_=z(2), func=AF.Copy), [zk], ["s_vt"])
            for c in range(4):
                P.op("dve", lambda e, c=c: e.tensor_scalar(out=v4(FT)[:, c, :], in0=v4(TH2)[:, c, :], scalar1=C1(c), scalar2=C2(c), op0=ALU.mult, op1=ALU.add),
                     ["s_th2"] + DCK, ["s_ft"])
                P.op("dve", lambda e, c=c: e.tensor_scalar(out=v4(KT)[:, c, :], in0=v4(TH2)[:, c, :], scalar1=HOML(c), scalar2=HOML(c), op0=ALU.mult, op1=ALU.add),
                     ["s_th2"] + DCK, ["s_kt"])
            P.op("dve", lambda e: e.scalar_tensor_tensor(out=GLU, in0=TH3, scalar=1.0, in1=z(4), op0=ALU.add, op1=ALU.mult), ["s_th3", zk], ["s_glu"])
            P.op("dve", lambda e: e.tensor_scalar(out=GLU, in0=GLU, scalar1=0.5, scalar2=None, op0=ALU.mult), ["s_glu"], ["s_glu"])
            for src, skey, dst, dkey in ((GLU, "s_glu", glu_tok, "gl_out"),):
                bi, bk = pm_alloc()
                for c in range(4):
                    P.op("pe", lambda e, c=c, bi=bi, src=src: e.transpose(out=pmm[0:16, bi, c * 128:(c + 1) * 128], in_=v4(src)[:, c, :], identity=ident_f[:]),
                         [skey, "ident_f"], [bk], signal=(c == 3))
                P.op("act", lambda e, bi=bi, dst=dst: e.activation(out=dst, in_=pmm[0:16, bi, :], func=AF.Copy), [bk], [dkey])
            dma(scs[:, 0:29, :], scd[:, 1:30, :], [], [], "o_scs0", is_out=True)
            dma(scs[:, 29, :], glu_tok, ["gl_out"], [], "o_scs1", is_out=True)

        H4_ = [0, 1, 2, 3]
        SB1 = dict(Sin=Sin, SinK=[["s_Sin%d" % i] for i in range(3)], X=Xc, XK=[["s_X0"], ["s_X1"]], vmk=vmk, vmkK=["s_vm"],
                   ktok=k_tok, ktokK=["s_ktok"], vtok=v_tok, vtokK=["s_vtok"], tag="a")
        SB2 = dict(Sin=[LF[:], BB[:], EN[:]], SinK=[["lf"], ["bb"], ["en"]],
                   X=[r_sq[:].rearrange("p c t -> p (c t)")[0:30, :], r_th2[:].rearrange("p c t -> p (c t)")[0:30, :]],
                   XK=[["sq%d" % c for c in H4_], ["th2_%d" % c for c in H4_]],
                   vmk=k_end[:].rearrange("p r f -> p (r f)").bitcast(F32)[0:16, :], vmkK=["kend%d_%d" % (t, c) for t in range(2) for c in H4_],
                   ktok=v_t[:].rearrange("p r f -> p (r f)").bitcast(F32)[0:16, :], ktokK=["v0", "v1"],
                   vtok=q_dec[:].rearrange("p r c t -> p (r c t)").bitcast(F32)[0:16, :], vtokK=["qdec%d_%d" % (t, c) for t in range(2) for c in H4_], tag="b")

        def sample_tok(SB):
            for src, skey, dst, dkeys in ((KT, "s_kt", SB["ktok"], SB["ktokK"]), (VT, "s_vt", SB["vtok"], SB["vtokK"])):
                bi, bk = pm_alloc()
                for c in range(4):
                    P.op("pe", lambda e, c=c, bi=bi, src=src: e.transpose(out=pmm[0:16, bi, c * 128:(c + 1) * 128], in_=v4(src)[:, c, :], identity=ident_f[:]),
                         [skey, "ident_f"], [bk], signal=(c == 3))
                P.op("act", lambda e, bi=bi, dst=dst: e.activation(out=dst, in_=pmm[0:16, bi, :], func=AF.Copy), [bk], dkeys)

        def sample_load(SB, b):
            t = SB["tag"]
            dma(SB["Sin"][b % 3], shd[b].rearrange("h d v -> d h v"), [], SB["SinK"][b % 3], "s_in%s%d" % (t, b % 3))
            dma(SB["X"][b % 2], scd[b], [], SB["XK"][b % 2], "s_x%s%d" % (t, b % 2))

        def sample_b(SB, b, first, last):
            i = b % 3
            xi = b % 2
            t = SB["tag"]
            Si, SiK, Xi, XiK = SB["Sin"][i], SB["SinK"][i], SB["X"][xi], SB["XK"][xi]
            vm, kt, vt = SB["vmk"], SB["ktok"], SB["vtok"]
            if first:
                sample_load(SB, b)
            if not last:
                sample_load(SB, b + 1)
            P.op("dve", lambda e: e.tensor_scalar(out=vm, in0=vt, scalar1=ident_f[0:16, b:b + 1], scalar2=None, op0=ALU.mult), SB["vtokK"] + ["ident_f"], SB["vmkK"])
            kb, kk = pm_alloc()
            for hh in range(4):
                P.op("pe", lambda e, hh=hh: e.matmul(pmm[:, kb, hh * 128:(hh + 1) * 128], lhsT=kt[:, hh * 128:(hh + 1) * 128], rhs=vm[:, hh * 128:(hh + 1) * 128],
                                                     start=True, stop=True), SB["ktokK"] + SB["vmkK"], [kk], signal=(hh == 3))
            for hh in range(4):
                P.op("dve", lambda e, hh=hh: e.scalar_tensor_tensor(out=Si[:, hh, :], in0=Si[:, hh, :], scalar=v4(FT)[:, hh, b:b + 1], in1=pmm[:, kb, hh * 128:(hh + 1) * 128],
                                                                    op0=ALU.mult, op1=ALU.add), ["s_ft", kk], SiK)
            dma(shs[b].rearrange("h d v -> d h v"), Si, SiK, [], "s_out%s%d" % (t, i), is_out=True)
            ob_, ok2 = pm_alloc()
            for hh in range(4):
                P.op("pe", lambda e, hh=hh: e.matmul(pmm[:, ob_, hh:hh + 1], lhsT=Si[:, hh, :], rhs=v4(QT)[:, hh, b:b + 1], start=True, stop=True),
                     SiK + ["s_qt"], [ok2], signal=(hh == 3))
            P.op("act", lambda e: e.activation(out=v4(OT)[:, :, b], in_=pmm[:, ob_, 0:4], func=AF.Copy), [ok2], ["s_ot"])
            P.op("dve", lambda e: e.tensor_tensor(out=Xi, in0=Xi, in1=cwrows[0:30, :], op=ALU.mult), ["cwrows"], XiK)
            db_, dk2 = pm_alloc()
            for c in range(4):
                P.op("pe", lambda e, c=c: e.matmul(pmm[:, db_, c:c + 1], lhsT=Xi[:, c * 128:(c + 1) * 128], rhs=ones_f[0:30, 0:1], start=True, stop=True),
                     XiK + ["ones_f"], [dk2], signal=(c == 3))
            P.op("act", lambda e: e.activation(out=v4(DWT)[:, :, b], in_=pmm[:, db_, 0:4], func=AF.Copy), [dk2], ["s_dwt"])

        def smp_A(SB, b, nxt_b):
            i, xi, t = b % 3, b % 2, SB["tag"]
            dma(SB["Sin"][i], shd[b].rearrange("h d v -> d h v"), [], SB["SinK"][i], "s_in%s%d" % (t, i))
            P.op(aux["eng"], lambda e: e.tensor_scalar(out=SB["vmk"], in0=SB["vtok"], scalar1=ident_f[0:16, b:b + 1], scalar2=1.0, op0=ALU.mult, op1=ALU.mult),
                 SB["vtokK"] + ["ident_f"], SB["vmkK"])
            Xi, XiK = SB["X"][xi], SB["XK"][xi]
            P.op(aux["eng"], lambda e: e.tensor_tensor(out=Xi, in0=Xi, in1=cwrows[0:30, :], op=ALU.mult), ["cwrows"], XiK)
            if nxt_b is not None:
                dma(SB["X"][nxt_b % 2], scd[nxt_b], [], SB["XK"][nxt_b % 2], "s_x%s%d" % (t, nxt_b % 2))

        def smp_B(SB, b):
            i, xi, t = b % 3, b % 2, SB["tag"]
            Si, SiK, Xi, XiK = SB["Sin"][i], SB["SinK"][i], SB["X"][xi], SB["XK"][xi]
            vm, kt = SB["vmk"], SB["ktok"]
            kb, kk = pm_alloc()
            for hh in range(4):
                P.op("pe", lambda e, hh=hh: e.matmul(pmm[:, kb, hh * 128:(hh + 1) * 128], lhsT=kt[:, hh * 128:(hh + 1) * 128], rhs=vm[:, hh * 128:(hh + 1) * 128],
                                                     start=True, stop=True), SB["ktokK"] + SB["vmkK"], [kk], signal=(hh == 3))
            for hh in range(4):
                P.op("dve", lambda e, hh=hh: e.scalar_tensor_tensor(out=Si[:, hh, :], in0=Si[:, hh, :], scalar=v4(FT)[:, hh, b:b + 1], in1=pmm[:, kb, hh * 128:(hh + 1) * 128],
                                                                    op0=ALU.mult, op1=ALU.add), ["s_ft", kk], SiK)
            dma(shs[b].rearrange("h d v -> d h v"), Si, SiK, [], "s_out%s%d" % (t, i), is_out=True)
            db_, dk2 = pm_alloc()
            for c in range(4):
                P.op("pe", lambda e, c=c: e.matmul(pmm[:, db_, c:c + 1], lhsT=Xi[:, c * 128:(c + 1) * 128], rhs=ones_f[0:30, 0:1], start=True, stop=True),
                     XiK + ["ones_f"], [dk2], signal=(c == 3))
            P.op("act", lambda e: e.activation(out=v4(DWT)[:, :, b], in_=pmm[:, db_, 0:4], func=AF.Copy), [dk2], ["s_dwt"])

        def smp_D(SB, b):
            i = b % 3
            Si, SiK = SB["Sin"][i], SB["SinK"][i]
            ob_, ok2 = pm_alloc()
            for hh in range(4):
                P.op("pe", lambda e, hh=hh: e.matmul(pmm[:, ob_, hh:hh + 1], lhsT=Si[:, hh, :], rhs=v4(QT)[:, hh, b:b + 1], start=True, stop=True),
                     SiK + ["s_qt"], [ok2], signal=(hh == 3))
            P.op("act", lambda e: e.activation(out=v4(OT)[:, :, b], in_=pmm[:, ob_, 0:4], func=AF.Copy), [ok2], ["s_ot"])

        def sample_pipe(SB, bs):
            st = {"n": 0}
            t = SB["tag"]

            def step():
                n = st["n"]
                st["n"] += 1
                if n == 0 and bs:
                    dma(SB["X"][bs[0] % 2], scd[bs[0]], [], SB["XK"][bs[0] % 2], "s_x%s%d" % (t, bs[0] % 2))
                if 0 <= n - 2 < len(bs):
                    smp_D(SB, bs[n - 2])
                if 0 <= n - 1 < len(bs):
                    smp_B(SB, bs[n - 1])
                if n < len(bs):
                    smp_A(SB, bs[n], bs[n + 1] if n + 1 < len(bs) else None)
                return n + 1 >= len(bs) + 2
            return step

        def sample_part2_stages(sWO):
          def S1():
            P.op("act", lambda e: e.activation(out=SQ, in_=OT, func=AF.Square), ["s_ot"], ["s_sq"])

          def S2():
            bi, bk = pm_alloc()
            P.op("pe", lambda e: e.matmul(pmm[:, bi, 0:64], lhsT=ones_f[:], rhs=SQ, start=True, stop=True), ["ones_f", "s_sq"], [bk])
            P.op("act", lambda e: e.activation(out=RS, in_=pmm[:, bi, 0:64], func=AF.Ln, scale=1.0 / 128, bias=EPS_AP), [bk, "cols1"], ["s_rs"])
            P.op("act", lambda e: e.activation(out=RS, in_=RS, func=AF.Exp, scale=-0.5), ["s_rs"], ["s_rs"])
            P.op("dve", lambda e: e.tensor_tensor(out=TMP, in0=OT, in1=RS, op=ALU.mult), ["s_ot", "s_rs"], ["s_tmp"])
            P.op("dve", lambda e: e.tensor_tensor(out=mixT[:, 0:4, 512:528], in0=v4(TMP), in1=v4(GT), op=ALU.mult), ["s_tmp", "s_gt"], ["mixhg4"])
            for c in range(4):
                P.op("dve", lambda e, c=c: e.scalar_tensor_tensor(out=v4(DWS)[:, c, :], in0=v4(GLU)[:, c, :], scalar=cwT[:, c, 30:31], in1=v4(DWT)[:, c, :], op0=ALU.mult, op1=ALU.add),
                     ["s_glu", "cwT", "s_dwt"], ["s_dws"])
                P.op("dve", lambda e, c=c: e.tensor_scalar(out=v4(DWS)[:, c, :], in0=v4(DWS)[:, c, :], scalar1=cols[:, 25 + c:26 + c], scalar2=None, op0=ALU.add), ["s_dws", "cols"], ["s_dws"])
            P.op("act", lambda e: e.activation(out=SQ, in_=DWS, func=AF.Square), ["s_dws"], ["s_sq"])

          def S3():
            ab, ak = pm_alloc()
            for c in range(4):
                P.op("pe", lambda e, c=c: e.matmul(pmm[:, ab, 0:16], lhsT=ones_f[:], rhs=v4(DWS)[:, c, :], start=(c == 0), stop=(c == 3)), ["ones_f", "s_dws"], [ak], signal=(c == 3))
            qb2, qk2 = pm_alloc()
            for c in range(4):
                P.op("pe", lambda e, c=c: e.matmul(pmm[:, qb2, 0:16], lhsT=ones_f[:], rhs=v4(SQ)[:, c, :], start=(c == 0), stop=(c == 3)), ["ones_f", "s_sq"], [qk2], signal=(c == 3))
            mean, msq, rstd, nmr = ln_t[:, 0, 0:16], ln_t[:, 1, 0:16], ln_t[:, 2, 0:16], ln_t[:, 3, 0:16]
            P.op("dve", lambda e: e.tensor_scalar(out=mean, in0=pmm[:, ab, 0:16], scalar1=1.0 / 512, scalar2=None, op0=ALU.mult), [ak], ["ovlB0"])
            P.op("dve", lambda e: e.tensor_tensor(out=msq, in0=mean, in1=mean, op=ALU.mult), ["ovlB0"], ["ovlB0"])
            P.op("dve", lambda e: e.scalar_tensor_tensor(out=msq, in0=pmm[:, qb2, 0:16], scalar=1.0 / 512, in1=msq, op0=ALU.mult, op1=ALU.subtract), [qk2, "ovlB0"], ["ovlB0"])
            P.op("act", lambda e: e.activation(out=rstd, in_=msq, func=AF.Ln, bias=EPS_AP), ["ovlB0", "cols1"], ["ovlB1"])
            P.op("act", lambda e: e.activation(out=rstd, in_=rstd, func=AF.Exp, scale=-0.5), ["ovlB1"], ["ovlB1"])
            P.op("dve", lambda e: e.tensor_tensor(out=nmr, in0=mean, in1=rstd, op=ALU.mult), ["ovlB0", "ovlB1"], ["ovlB1"])
            for c in range(4):
                P.op("dve", lambda e, c=c: e.tensor_tensor(out=v4(TMP)[:, c, :], in0=v4(DWS)[:, c, :], in1=rstd, op=ALU.mult), ["s_dws", "ovlB1"], ["s_tmp"])
                P.op("dve", lambda e, c=c: e.tensor_tensor(out=v4(TMP)[:, c, :], in0=v4(TMP)[:, c, :], in1=nmr, op=ALU.subtract), ["s_tmp", "ovlB1"], ["s_tmp"])
                P.op("act", lambda e, c=c: e.activation(out=mixT[:, 4 + c, 512:528], in_=v4(TMP)[:, c, :], func=AF.Silu, scale=cols[:, 29 + c:30 + c], bias=cols[:, 33 + c:34 + c]),
                     ["s_tmp", "cols"], ["mixcv4"])

          def S4():
            wout_tile(4, 16, sWO, ["mixhg4", "mixcv4"])
          return [S1, S2, S3, S4]

        def sample_part2(sWO):
            for f in sample_part2_stages(sWO):
                f()

        def sample_barrier():
            P.op("dve", lambda e: e.memset(stat[:, 63:64], 0.0), [], SMK + ["ovl0a", "ovl0b", "ovl1", "statx"])

        next_parts, next_slots = None, None
        for seg in range(nseg):
            if seg == 0:
                slots = {"QF": load_win(0, seg), "IG": load_win(1, seg), "AB": load_win(2, seg)}
                parts = [tile_parts(seg, j, slots, last_tile=(seg == nseg - 1 and j == 3)) for j in range(4)]
            else:
                slots, parts = next_slots, next_parts
            sQF, sIG, sAB = slots["QF"], slots["IG"], slots["AB"]
            sWO = load_wout(seg)
            if seg == 0:
                build_diag()
            smp = with_sample and seg == 0
            smp_last = with_sample and seg == nseg - 1
            p2segs = list(range(0, nseg - 1))
            my_b = []
            if with_sample and seg in p2segs:
                cnt = [NSMP] if len(p2segs) == 1 else ([2] + [-(-(NSMP - 2) // (len(p2segs) - 1))] * (len(p2segs) - 1))
                k0 = sum(cnt[:seg])
                my_b = list(range(k0, min(NSMP, k0 + cnt[seg])))
            p1_b = list(range(NSMP)) if (with_sample and nseg == 1) else []
            if seg == 0:
                meta_prologue(sQF, sIG, sAB)
            if dbg_stop == "meta":
                break
            if smp:
                sample_part1(sQF, sIG, sAB)
            pcv = preconv_chunks() if (seg == 0 and nseg > 1 and dbg_stop is None) else []
            for f in pcv:
                f()
            pcv = []
            pm_state["banks"] = [2, 3, 4, 5]
            aux["eng"] = "dve" if seg == 0 else "pool"
            parts[0]["A0a"]()
            parts[0]["A0b"]()
            parts[1]["A0a"]()
            run_step(None, parts[0], parts[1], None, None, parts[2])
            for j in range(4):
                fl = None
                if j == 3:
                    conv_rest = conv_chunks()
                    if not p1_b:
                        fl, conv_rest = conv_rest[:4], conv_rest[4:]
                run_step(parts[j], parts[j + 1] if j < 3 else None, parts[j + 2] if j < 2 else None, parts[j - 1] if j > 0 else None,
                         fl, parts[j + 3] if j < 1 else None)
                if p1_b:
                    if j == 0:
                        sample_tok(SB1)
                    for b in range(4 * j, 4 * j + 4):
                        sample_b(SB1, b, b == 0, b == NSMP - 1)
                for f in pcv[:14]:
                    f()
                pcv = pcv[14:]
            parts[3]["B4b"]()
            if seg == 0 and preconv_done:
                P.op("dve", lambda e: e.memset(stat[:, 62:63], 0.0), [], ["bnc%d" % i for i in range(4)] + ["hn2T%d" % j for j in range(4)] + ["staty"])
            pm_state["banks"] = [0, 1, 2, 3, 4, 5]
            if p1_b:
                sample_barrier()
            seg_1bc(seg, sWO, seg == nseg - 1, conv_rest, sample_part2_stages(sWO) if smp_last else None)
            qslots = {}

            def ensure_q(qq):
                if qq <= 3 and qq not in qslots:
                    qslots[qq] = (load_wup(qq, seg), load_wdown(qq, seg))

            def cached(qq):
                return qq <= 3
            for q in range(4):
                ensure_q(q)
                if cached(q + 1):
                    ensure_q(q + 1)
                sU, sD = qslots[q]
                hooks = None
                if q == 3 and seg + 1 < nseg:
                    next_slots = {"QF": load_win(0, seg + 1), "IG": load_win(1, seg + 1), "AB": None}
                    next_parts = [tile_parts(seg + 1, j, next_slots, last_tile=(seg + 1 == nseg - 1 and j == 3)) for j in range(4)]
                    hooks = {0: next_parts[0]["A0a"], 1: next_parts[0]["A0b"], 3: next_parts[1]["A0a"]}

                    def _ab(ns=next_slots, sg=seg + 1):
                        ns["AB"] = load_win(2, sg)
                    hooks["after_up"] = _ab
                if my_b and q == 0:
                    sample_tok(SB2)
                    pipe = sample_pipe(SB2, my_b)
                mlp_quarter(seg, q, sU, sD, 4, NSMP if smp_last else 0, hooks, pipe if my_b else None)
                if my_b and q == 3:
                    while not pipe():
                        pass
        dma(shp.rearrange("h d v -> d h v"), S[:], ["S0", "S1", "S2", "S3"], [], "o_shp", is_out=True)

        with nc.Block() as block:
            @block.sync
            def _(e):
                P.replay("sp", e)

            @block.scalar
            def _(e):
                P.replay("act", e)

            @block.vector
            def _(e):
                P.replay("dve", e)

            @block.gpsimd
            def _(e):
                P.replay("pool", e)

            @block.tensor
            def _(e):
                P.replay("pe", e)
    return nc


def _consts():
    ident = np.eye(128, dtype=np.float32)
    s = np.arange(128)[:, None]
    t = np.arange(128)[None, :]
    mask2 = ((s // 64 == t // 64) & (s <= t)).astype(np.float32)
    mscan = np.ones((128, 128), np.float32)
    mscan[:, 0] = 0.0
    mscan[:, 64] = 0.0
    return ident, mask2, mscan


def make_in_maps(inputs, nseg=4):
    f = lambda a: np.ascontiguousarray(np.asarray(a, dtype=np.float32))
    ident, mask2, mscan = _consts()
    rows = np.concatenate([
        f(inputs["hg_lb"]).reshape(8, 128), f(inputs["norm1_g"]).reshape(8, 128), f(inputs["norm2_g"]).reshape(8, 128),
        f(inputs["hg_onorm_g"]).reshape(1, 128), f(inputs["conv_b"]).reshape(4, 128), f(inputs["conv_ln_g"]).reshape(4, 128),
        f(inputs["conv_ln_b"]).reshape(4, 128)], axis=0)
    shared = {
        "meta": f(inputs["meta_tokens"]), "w_in": f(inputs["w_in"][0]), "w_out": f(inputs["w_out"][0]),
        "w_up": f(inputs["w_up"][0]), "w_down": f(inputs["w_down"][0]), "rows": f(rows), "conv_w": f(inputs["conv_w"][0]),
        "final_g": f(inputs["final_g"]).reshape(1, 1024), "ident": ident, "mask2": mask2, "mscan": mscan,
    }
    xp = f(inputs["x_prompt"])
    xs = f(inputs["x_sample"])
    sh = f(inputs["state_hgrn"])
    sc = f(inputs["state_conv"])
    maps = []
    for c in range(xp.shape[0]):
        m = dict(shared)
        m["xp"] = xp[c, :nseg * 512]
        m["xs"] = xs[c * NSMP:(c + 1) * NSMP, 0]
        m["sh"] = sh[0, c * NSMP:(c + 1) * NSMP]
        m["sc"] = sc[0, c * NSMP:(c + 1) * NSMP]
        maps.append(m)
    return maps


_NC_CACHE = {}


def kernel(**inputs):
    nseg = SEQ // 512
    if nseg not in _NC_CACHE:
        _NC_CACHE[nseg] = build_program(nseg)
    nc = _NC_CACHE[nseg]
    maps = make_in_maps(inputs, nseg)
    res = run_bass_kernel_spmd(nc, maps, core_ids=list(range(NCORES)))
    r = res.results
    y_prompt = np.stack([r[c]["yp"] for c in range(NCORES)], axis=0)
    y_sample = np.concatenate([r[c]["ys"] for c in range(NCORES)], axis=0)[:, None, :]
    shp = np.stack([r[c]["shp"] for c in range(NCORES)], axis=0)[None]
    scp = np.stack([r[c]["scp"] for c in range(NCORES)], axis=0)[None]
    shs = np.concatenate([r[c]["shs"] for c in range(NCORES)], axis=0)[None]
    scs = np.concatenate([r[c]["scs"] for c in range(NCORES)], axis=0)[None]
    return (y_prompt.astype(np.float32), y_sample.astype(np.float32), shp.astype(np.float32), scp.astype(np.float32),
            shs.astype(np.float32), scs.astype(np.float32))
```

```python
import numpy as np
from contextlib import ExitStack
import concourse.bass as bass
import concourse.mybir as mybir
from concourse.bass_utils import run_bass_kernel_spmd

F32 = mybir.dt.float32
BF16 = mybir.dt.bfloat16
AF = mybir.ActivationFunctionType
ALU = mybir.AluOpType
EPS = 1e-6
NCORES = 8
SEQ = 2048
NSMP = 16
ENGS = ("pe", "act", "dve", "pool", "sp")
CAST_ENG = ("pool", "act", "dve", "pool", "act", "pool", "act", "dve")


class Prog:
    def __init__(self, nc, stack):
        self.nc = nc
        self.stack = stack
        self.ops = {e: [] for e in ENGS}
        self.cnt = {e: 0 for e in ENGS}
        self.sem = {e: stack.enter_context(nc.semaphore("sem_" + e)) for e in ENGS}
        self.semname = {id(self.sem[e]): e for e in ENGS}
        self.waited = {e: {} for e in ENGS}
        self.last_w = {}
        self.readers = {}
        self.dma_sems = {}
        self.dma_cnt = {}
        self.out_dma = []

    def dsem(self, name):
        if name not in self.dma_sems:
            self.dma_sems[name] = self.stack.enter_context(self.nc.semaphore("dq_" + name))
            self.dma_cnt[name] = 0
        return self.dma_sems[name]

    def op(self, eng, fn, reads=(), writes=(), signal=True, dma=None, is_out=False):
        deps = {}

        def add(h):
            s, v = h
            k = id(s)
            if k not in deps or deps[k][1] < v:
                deps[k] = (s, v)

        for k in reads:
            if k in self.last_w:
                add(self.last_w[k])
        for k in writes:
            if k in self.last_w:
                add(self.last_w[k])
            for h in self.readers.get(k, {}).values():
                add(h)
        waits = []
        wd = self.waited[eng]
        for k, (s, v) in deps.items():
            if eng == "pe" and s is self.sem["pe"]:
                continue
            if wd.get(k, 0) < v:
                wd[k] = v
                waits.append((s, v))
        if dma is not None:
            s = self.dsem(dma)
            self.dma_cnt[dma] += 16
            h = (s, self.dma_cnt[dma])
            inc = (s, 16)
            if is_out:
                self.out_dma.append(dma)
        elif signal:
            self.cnt[eng] += 1
            h = (self.sem[eng], self.cnt[eng])
            inc = (self.sem[eng], 1)
        else:
            h = (self.sem[eng], self.cnt[eng] + 1)
            inc = None
        for k in reads:
            r = self.readers.setdefault(k, {})
            kk = id(h[0])
            if kk not in r or r[kk][1] < h[1]:
                r[kk] = h
        for k in writes:
            self.last_w[k] = h
            self.readers[k] = {}
        self.ops[eng].append((waits, fn, inc))

    def replay(self, eng, e):
        for waits, fn, inc in self.ops[eng]:
            for s, v in waits:
                e.wait_ge(s, v)
            ins = fn(e)
            if inc is not None:
                ins.then_inc(inc[0], inc[1])
        if eng == "sp":
            for name in dict.fromkeys(self.out_dma):
                e.wait_ge(self.dma_sems[name], self.dma_cnt[name])


def build_program(nseg=4, with_sample=True, dbg_stop=None):
    nc = bass.Bass("TRN2", target_bir_lowering=False, dynamic_dma_scratch_size=2048)
    T = nseg * 512

    def din(name, shape):
        return nc.dram_tensor(name, shape, F32, kind="ExternalInput").ap()

    def dout(name, shape):
        return nc.dram_tensor(name, shape, F32, kind="ExternalOutput").ap()

    xp = din("xp", [T, 1024])
    xsd = din("xs", [NSMP, 1024])
    shd = din("sh", [NSMP, 4, 128, 128])
    scd = din("sc", [NSMP, 30, 512])
    metad = din("meta", [16, 1024])
    w_in = din("w_in", [1024, 3072])
    w_out = din("w_out", [1024, 1024])
    w_up = din("w_up", [1024, 4096])
    w_down = din("w_down", [4096, 1024])
    rowsd = din("rows", [37, 128])
    cwd = din("conv_w", [31, 512])
    fgd = din("final_g", [1, 1024])
    identd = din("ident", [128, 128])
    mask2d = din("mask2", [128, 128])
    mscand = din("mscan", [128, 128])
    yp = dout("yp", [T, 1024])
    ysd = dout("ys", [NSMP, 1024])
    shp = dout("shp", [4, 128, 128])
    scp = dout("scp", [30, 512])
    shs = dout("shs", [NSMP, 4, 128, 128])
    scs = dout("scs", [NSMP, 30, 512])
    wcache = nc.dram_tensor("wcache", [12, 128, 8192], BF16, kind="Internal").ap()

    st = ExitStack()
    with st:
        P = Prog(nc, st)

        def sb(name, shape, dt=F32):
            return st.enter_context(nc.sbuf_tensor(name, shape, dt))

        def ps(name, shape, dt=F32):
            return st.enter_context(nc.psum_tensor(name, shape, dt))

        h = sb("h", [128, 5, 1024])
        hn2T = sb("hn2T", [128, 8, 528], BF16)
        arena = sb("arena", [128, 4, 8, 1024], BF16)
        stg = sb("stg", [128, 2, 1024])
        diag = sb("diag", [128, 4, 31, 128], BF16)
        ident_b = sb("ident_b", [128, 128], BF16)
        ident_f = sb("ident_f", [128, 128])
        mask2 = sb("mask2s", [128, 128])
        mscan = sb("mscans", [128, 128])
        ones_f = sb("ones_f", [128, 128])
        rows = sb("rowss", [37, 128])
        cols = sb("cols", [128, 40])
        dc = sb("dcols", [128, 32])
        cwrows = sb("cwrows", [31, 512])
        cwT = sb("cwT", [128, 4, 32])
        fgrow = sb("fgrow", [1, 1024])
        fg_bc = sb("fg_bc", [128, 1024])
        S = sb("S", [128, 4, 128])
        S_bf = sb("S_bf", [128, 2, 4, 128], BF16)
        xs_b = sb("xs_b", [128, 2, 1024], BF16)
        hnT = sb("hnT", [128, 2, 8, 128], BF16)
        r_th2 = sb("r_th2", [128, 4, 128])
        r_sq = sb("r_sq", [128, 4, 128])
        LF = sb("LF", [128, 4, 128])
        BB = sb("BB", [128, 4, 128])
        EN = sb("EN", [128, 4, 128])
        q_dec = sb("q_dec", [128, 2, 4, 128], BF16)
        k_inv = sb("k_inv", [128, 2, 4, 128], BF16)
        k_endT = sb("k_endT", [128, 4, 128], BF16)
        k_end = sb("k_end", [128, 2, 512], BF16)
        v_t = sb("v_t", [128, 2, 512], BF16)
        ovlB = sb("ovlB", [128, 1024])
        r_th3 = sb("r_th3", [128, 2, 128])
        dec = sb("dec", [128, 2, 4, 2])
        scm = sb("scm", [128, 4, 128], BF16)
        y_hg = sb("y_hg", [128, 2, 512], BF16)
        stat = sb("stat", [128, 64])
        convbuf = sb("convbuf", [128, 4, 542], BF16)
        mixT = sb("mixT", [128, 8, 528], BF16)
        ovlA = sb("ovlA", [128, 4096])
        glu_last = sb("glu_last", [128, 4, 32])
        gl_out = sb("gl_out", [32, 512])
        hnTs = sb("hnTs", [128, 8, 16], BF16)
        uTs = sb("uTs", [128, 2, 8, 16], BF16)
        sm = sb("sm", [128, 14, 64])
        print('SBUF bytes remaining', nc.sbuf_bytes_remaining)
        ptr = ps("ptr", [128, 2, 1024], BF16)
        pmm = ps("pmm", [128, 6, 512])

        gate = ovlB[:].rearrange("p (r f) -> p r f", r=2)
        ln_t = ovlB[:].rearrange("p (a f) -> p a f", a=4)
        rbuf = gate
        dw = ovlA[:, 0:2048].rearrange("p (c f) -> p c f", c=4)
        dwsq = ovlA[:, 2048:3072].rearrange("p (r f) -> p r f", r=2)
        t1r = ovlA[:, 3072:4096].rearrange("p (r f) -> p r f", r=4)
        uT = [ovlA[:, i * 2048:(i + 1) * 2048].bitcast(BF16).rearrange("p (c f) -> p c f", c=8) for i in range(2)]

        ring = {}

        def nxt(name, n):
            ring[name] = (ring.get(name, -1) + 1) % n
            return ring[name]

        pm_state = {"banks": [0, 1, 2, 3, 4, 5], "i": 0}

        def pm_alloc():
            b = pm_state["banks"][pm_state["i"] % len(pm_state["banks"])]
            pm_state["i"] += 1
            return b, "pmm%d" % b

        def pm_pair():
            assert len(pm_state["banks"]) == 6
            while pm_state["banks"][pm_state["i"] % 6] % 2 == 1:
                pm_state["i"] += 1
            b = pm_state["banks"][pm_state["i"] % 6]
            pm_state["i"] += 2
            return b, ["pmm%d" % b, "pmm%d" % (b + 1)]

        def ptr_alloc():
            i = nxt("ptr", 2)
            return i, "ptr%d" % i

        def stat_alloc(w=1):
            i = nxt("stat", 16)
            return stat[:, i * 4:i * 4 + w], "stat%d" % i

        def dma(out, in_, reads, writes, ring_name, is_out=False, eng="sp"):
            P.op(eng, lambda e, o=out, i=in_: e.dma_start(out=o, in_=i), reads=reads, writes=writes,
                 dma=ring_name, is_out=is_out)

        dma(ident_f[:], identd, [], ["ident_f"], "c0")
        dma(mask2[:], mask2d, [], ["mask2"], "c1")
        dma(mscan[:], mscand, [], ["mscan"], "c2")
        dma(rows[:], rowsd, [], ["rows"], "c3")
        dma(cwrows[:], cwd, [], ["cwrows"], "c4")
        dma(fgrow[:], fgd, [], ["fgrow"], "c5")
        P.op("dve", lambda e: e.tensor_copy(out=ident_b[:], in_=ident_f[:]), ["ident_f"], ["ident_b"])
        P.op("dve", lambda e: e.memset(ones_f[:], 1.0), [], ["ones_f"])
        P.op("dve", lambda e: e.memset(S[:], 0.0), [], ["S0", "S1", "S2", "S3"])
        P.op("dve", lambda e: e.memset(S_bf[:], 0.0), [], ["Sbf%d_%d" % (c, i) for c in range(4) for i in range(2)])
        P.op("dve", lambda e: e.memset(convbuf[:], 0.0), [], ["cb%d" % c for c in range(4)])
        bi, bk = pm_alloc()
        P.op("pe", lambda e, b=bi: e.transpose(out=pmm[:, b, 0:37], in_=rows[:], identity=ident_f[0:37, 0:37]),
             ["rows", "ident_f"], [bk])
        P.op("dve", lambda e, b=bi: e.tensor_copy(out=cols[:, 0:37], in_=pmm[:, b, 0:37]), [bk], ["cols"])
        P.op("dve", lambda e: e.memset(cols[:, 37:38], 1.0), [], ["cols1"])
        ONE = cols[:, 37:38]
        for c in range(4):
            bi, bk = pm_alloc()
            P.op("pe", lambda e, b=bi, c=c: e.transpose(out=pmm[:, b, 0:31], in_=cwrows[:, c * 128:(c + 1) * 128],
                                                         identity=ident_f[0:31, 0:31]), ["cwrows", "ident_f"], [bk])
            P.op("dve", lambda e, b=bi, c=c: e.tensor_copy(out=cwT[:, c, 0:31], in_=pmm[:, b, 0:31]), [bk], ["cwT"])
        for n in range(2):
            bi, bk = pm_alloc()
            P.op("pe", lambda e, b=bi, n=n: e.matmul(pmm[:, b, :], lhsT=ones_f[0:1, :], rhs=fgrow[0:1, n * 512:(n + 1) * 512],
                                                     start=True, stop=True), ["ones_f", "fgrow"], [bk])
            P.op("dve", lambda e, b=bi, n=n: e.tensor_copy(out=fg_bc[:, n * 512:(n + 1) * 512], in_=pmm[:, b, :]), [bk], ["fg_bc"])
        P.op("dve", lambda e: e.tensor_tensor(out=dc[:, 24:28], in0=cols[:, 0:4], in1=cols[:, 4:8], op=ALU.subtract), ["cols"], ["dc_d"])
        P.op("act", lambda e: e.activation(out=dc[:, 0:4], in_=dc[:, 24:28], func=AF.Tanh, scale=0.5), ["dc_d"], ["dc_t"])
        P.op("dve", lambda e: e.tensor_scalar(out=dc[:, 4:8], in0=dc[:, 0:4], scalar1=-0.5, scalar2=0.5, op0=ALU.mult, op1=ALU.add), ["dc_t"], ["dc_a"])
        P.op("dve", lambda e: e.tensor_scalar(out=dc[:, 8:12], in0=dc[:, 0:4], scalar1=0.25, scalar2=-0.25, op0=ALU.mult, op1=ALU.add), ["dc_t"], ["dc_b"])
        P.op("dve", lambda e: e.tensor_scalar(out=dc[:, 12:16], in0=dc[:, 0:4], scalar1=0.25, scalar2=0.75, op0=ALU.mult, op1=ALU.add), ["dc_t"], ["dc_c"])
        P.op("dve", lambda e: e.tensor_scalar(out=dc[:, 16:20], in0=dc[:, 0:4], scalar1=-0.25, scalar2=0.25, op0=ALU.mult, op1=ALU.add), ["dc_t"], ["dc_e"])
        P.op("act", lambda e: e.activation(out=dc[:, 20:24], in_=dc[:, 16:20], func=AF.Ln), ["dc_e"], ["dc_f"])
        DCK = ["dc_a", "dc_b", "dc_c", "dc_e", "dc_f"]
        C1 = lambda c: dc[:, 8 + c:9 + c]
        C2 = lambda c: dc[:, 12 + c:13 + c]
        HOML = lambda c: dc[:, 16 + c:17 + c]
        LNHOML = lambda c: dc[:, 20 + c:21 + c]
        def build_diag():
          for c in range(4):
            for j in range(31):
                P.op("dve", lambda e, c=c, j=j: e.tensor_scalar(out=diag[:, c, j, :], in0=ident_b[:], scalar1=cwT[:, c, j:j + 1],
                                                                scalar2=0.5, op0=ALU.mult, op1=ALU.mult),
                     ["ident_b", "cwT"], ["diag"])

        slot_ring = [0]

        preconv_done = set()

        def preconv_chunks():
            out = []
            for q, which in ((3, 0), (3, 1)):
                if True:
                    wid = 4 + 2 * q + which
                    preconv_done.add(wid)
                    for k in range(8):
                        if which == 0:
                            src = w_up[k * 128:(k + 1) * 128, q * 1024:(q + 1) * 1024]
                            sc = cols[:, 16 + k:17 + k]
                        else:
                            src = w_down[q * 1024 + k * 128:q * 1024 + (k + 1) * 128, :]
                            sc = ONE

                        def f(wid=wid, k=k, src=src, sc=sc):
                            i = nxt("stg", 2)
                            bi = nxt("bnc", 4)
                            bap = hn2T[:, 2 * bi:2 * bi + 2, 0:512]
                            dma(stg[:, i, :], src, [], ["stg%d" % i], "pstg%d" % i, eng="pool")
                            P.op("pool", lambda e: e.tensor_scalar(out=bap, in0=stg[:, i, :].rearrange("p (a f) -> p a f", a=2), scalar1=sc, scalar2=1.0,
                                                                   op0=ALU.mult, op1=ALU.mult), ["stg%d" % i, "cols", "cols1"], ["bnc%d" % bi])
                            dma(wcache[wid][:, k * 1024:(k + 1) * 1024].rearrange("p (a f) -> p a f", a=2), bap, ["bnc%d" % bi], ["wc%dk%d" % (wid, k)], "bncq%d" % bi, eng="pool")
                        out.append(f)
            return out

        def load_slot(src_fn, scale_fn, wid, seg):
            s = slot_ring[0] % 4
            slot_ring[0] += 1
            skeys = ["slot%dk%d" % (s, k) for k in range(8)]
            if seg > 0 or wid in preconv_done:
                dma(arena[:, s, :, :], wcache[wid].rearrange("p (k c) -> p k c", k=8), ["wc%d" % wid] + ["wc%dk%d" % (wid, k) for k in range(8)], skeys, "slotld%d" % s)
                return s
            for k in range(8):
                if slot_ring[0] <= 4:
                    i = nxt("stg7", 7)
                    if i < 2:
                        sap, gkeys, sring = stg[:, i, :], ["stg%d" % i], "stg%d" % i
                    else:
                        sap, gkeys, sring = h[:, i - 2, :], ["h%d" % (i - 2)], "xin%d" % (i - 2)
                else:
                    i = nxt("stg3", 3) if wid < 8 else nxt("stg4", 4)
                    if i < 2:
                        sap, gkeys, sring = stg[:, i, :], ["stg%d" % i], "stg%d" % i
                    elif i == 2:
                        sap, gkeys, sring = hnT[:].rearrange("p r k t -> p (r k t)").bitcast(F32), ["hnT0", "hnT1"], "stgx3"
                    else:
                        sap, gkeys, sring = xs_b[:].rearrange("p r f -> p (r f)").bitcast(F32), ["xs0", "xs1"], "stgx2"
                dma(sap, src_fn(k), [], gkeys, sring)
                ce = CAST_ENG[k] if slot_ring[0] <= 4 else "pool"
                sc = scale_fn(k)
                if ce == "act":
                    P.op("act", lambda e, s=s, k=k, sap=sap, sc=sc: e.activation(out=arena[:, s, k, :], in_=sap, func=AF.Copy, scale=sc),
                         gkeys + ["cols", "cols1"], ["slot%dk%d" % (s, k)])
                else:
                    P.op(ce, lambda e, s=s, k=k, sap=sap, sc=sc: e.tensor_scalar(out=arena[:, s, k, :], in0=sap, scalar1=sc, scalar2=1.0,
                                                                             op0=ALU.mult, op1=ALU.mult),
                         gkeys + ["cols", "cols1"], ["slot%dk%d" % (s, k)])
            if nseg > 1:
                dma(wcache[wid].rearrange("p (k c) -> p k c", k=8), arena[:, s, :, :], skeys, ["wc%d" % wid], "wcw%d" % wid)
            return s

        def load_win(part, seg):
            return load_slot(lambda k: w_in[k * 128:(k + 1) * 128, part * 1024:(part + 1) * 1024], lambda k: cols[:, 8 + k:9 + k], part, seg)

        def load_wout_pool():
            s = slot_ring[0] % 4
            slot_ring[0] += 1
            skeys = ["slot%dk%d" % (s, k) for k in range(8)]
            for k in range(8):
                i = nxt("stg", 2)
                sc = cols[:, 24:25] if k < 4 else ONE
                dma(stg[:, i, :], w_out[k * 128:(k + 1) * 128, :], [], ["stg%d" % i], "pstg%d" % i, eng="pool")
                P.op("pool", lambda e, k=k, i=i, sc=sc: e.tensor_scalar(out=arena[:, s, k, :], in0=stg[:, i, :], scalar1=sc, scalar2=1.0, op0=ALU.mult, op1=ALU.mult),
                     ["stg%d" % i, "cols", "cols1"], ["slot%dk%d" % (s, k)])
            if nseg > 1:
                dma(wcache[3].rearrange("p (k c) -> p k c", k=8), arena[:, s, :, :], skeys, ["wc3"], "wcw3")
            return s

        def load_wout(seg):
            if seg == 0 and nseg > 1 and dbg_stop is None:
                return load_wout_pool()
            return load_slot(lambda k: w_out[k * 128:(k + 1) * 128, :], lambda k: (cols[:, 24:25] if k < 4 else ONE), 3, seg)

        def load_wup(q, seg):
            return load_slot(lambda k: w_up[k * 128:(k + 1) * 128, q * 1024:(q + 1) * 1024], lambda k: cols[:, 16 + k:17 + k], 4 + 2 * q, seg)

        def load_wdown(q, seg):
            return load_slot(lambda k: w_down[q * 1024 + k * 128:q * 1024 + (k + 1) * 128, :], lambda k: ONE, 5 + 2 * q, seg)

        def rstd_from_ss(ss_ap, ss_key, n, inv_n):
            o, ok = stat_alloc(ss_ap.shape[1])
            P.op("act", lambda e: e.activation(out=o[0:n, :], in_=ss_ap, func=AF.Ln, scale=inv_n, bias=EPS_AP[0:n, :]), [ss_key, "cols1"], [ok])
            P.op("act", lambda e: e.activation(out=o[0:n, :], in_=o[0:n, :], func=AF.Exp, scale=-0.5), [ok], [ok])
            return o, ok

        P.op("dve", lambda e: e.memset(cols[:, 38:39], EPS), [], ["cols1"])
        EPS_AP = cols[:, 38:39]

        def front_a(x_ap, x_key, n):
            xi = nxt("xs", 2)
            ss, ssk = stat_alloc()
            P.op("act", lambda e: e.activation(out=xs_b[0:n, xi, :], in_=x_ap, func=AF.Square, accum_out=ss[0:n, :]),
                 [x_key], ["xs%d" % xi, ssk])
            rs, rsk = rstd_from_ss(ss[0:n, :], ssk, n, 1.0 / 1024)
            P.op("dve", lambda e: e.tensor_scalar(out=xs_b[0:n, xi, :], in0=x_ap, scalar1=rs[0:n, :], scalar2=None, op0=ALU.mult),
                 [x_key, rsk], ["xs%d" % xi])
            return xi

        def front_b(xi, n, dst_fn, dst_key):
            pi, pk = ptr_alloc()
            pv = ptr[:, pi, :].rearrange("p (k t) -> p k t", k=8)
            for k in range(8):
                P.op("pe", lambda e, k=k: e.transpose(out=pv[:, k, 0:n], in_=xs_b[0:n, xi, k * 128:(k + 1) * 128], identity=ident_b[0:n, 0:n]),
                     ["xs%d" % xi, "ident_b"], [pk], signal=(k == 7))
            P.op("act", lambda e: e.activation(out=dst_fn(), in_=pv[:, :, 0:n], func=AF.Copy), [pk], [dst_key])

        def front(x_ap, x_key, n, dst_fn, dst_key):
            front_b(front_a(x_ap, x_key, n), n, dst_fn, dst_key)

        def proj_fm(slot, co, c, rhs_fn, n, rkeys):
            bi, bk = pm_alloc()
            for k in range(8):
                P.op("pe", lambda e, k=k, b=bi: e.matmul(pmm[:, b, 0:n], lhsT=arena[:, slot, k, co + c * 128:co + (c + 1) * 128], rhs=rhs_fn(k),
                                                         start=(k == 0), stop=(k == 7)),
                     ["slot%dk%d" % (slot, k)] + rkeys, [bk], signal=(k == 7))
            return bi, bk

        def proj_tm(slot, co, lhs_fn, n, rkeys):
            bi, bk = pm_alloc()
            for k in range(8):
                P.op("pe", lambda e, k=k, b=bi: e.matmul(pmm[0:n, b, :], lhsT=lhs_fn(k), rhs=arena[:, slot, k, co:co + 512],
                                                         start=(k == 0), stop=(k == 7)),
                     ["slot%dk%d" % (slot, k)] + rkeys, [bk], signal=(k == 7))
            return bi, bk

        aux = {"eng": "dve"}

        def f_chain(bi, bk, c, n, ti, nch):
            P.op("act", lambda e: e.activation(out=r_th2[:, c, 0:n], in_=pmm[:, bi, 0:n], func=AF.Tanh, scale=-0.5), [bk], ["th2_%d" % c])
            P.op(aux["eng"], lambda e: e.tensor_scalar(out=LF[:, c, 0:n], in0=r_th2[:, c, 0:n], scalar1=C1(c), scalar2=C2(c), op0=ALU.mult, op1=ALU.add),
                 ["th2_%d" % c] + DCK, ["lf"])
            P.op(aux["eng"], lambda e: e.tensor_scalar(out=r_th2[:, c, 0:n], in0=r_th2[:, c, 0:n], scalar1=HOML(c), scalar2=HOML(c), op0=ALU.mult, op1=ALU.add),
                 ["th2_%d" % c] + DCK, ["th2_%d" % c])

        def g_wide(n, ti, nch, with_q):
            g_wide_a(n)
            g_wide_b(n, ti, nch, with_q)

        def g_wide_a(n):
            P.op("act", lambda e: e.activation(out=LF[:, :, 0:n], in_=LF[:, :, 0:n], func=AF.Ln), ["lf"], ["lf"])
            for c in range(4):
                P.op("dve", lambda e, c=c: e.tensor_tensor_scan(out=BB[:, c, 0:n], data0=mscan[:, 0:n], data1=LF[:, c, 0:n], initial=0.0,
                                                                op0=ALU.mult, op1=ALU.add), ["lf", "mscan"], ["bb"])

        def g_wide_b(n, ti, nch, with_q):
            H4 = [0, 1, 2, 3]
            P.op("act", lambda e: e.activation(out=EN[:, :, 0:n], in_=BB[:, :, 0:n], func=AF.Exp, scale=-1.0), ["bb"], ["en"])
            P.op("act", lambda e: e.activation(out=BB[:, :, 0:n], in_=BB[:, :, 0:n], func=AF.Exp), ["bb"], ["bb"])
            P.op("dve", lambda e: e.tensor_tensor(out=k_inv[:, ti, :, 0:n], in0=r_th2[:, :, 0:n], in1=EN[:, :, 0:n], op=ALU.mult),
                 ["th2_%d" % c for c in H4] + ["en"], ["kinv%d_%d" % (ti, c) for c in H4])
            if with_q:
                P.op("dve", lambda e: e.tensor_tensor(out=q_dec[:, ti, :, 0:n], in0=r_sq[:, :, 0:n], in1=BB[:, :, 0:n], op=ALU.mult),
                     ["sq%d" % c for c in H4] + ["bb"], ["qdec%d_%d" % (ti, c) for c in H4])
            cs = n // nch
            P.op("dve", lambda e: e.tensor_copy(out=dec[:, ti, :, 0:nch], in_=BB[:, :, cs - 1:n:cs]), ["bb"], ["dec%d_%d" % (ti, c) for c in H4])
            if nch == 2:
                kin = k_inv[:, ti, :, :].rearrange("p c (h s) -> p (c h) s", s=cs)
                kout = k_endT[:, :, :].rearrange("p c (h s) -> p (c h) s", s=cs)
                dbc = dec[:, ti, :, :].rearrange("p c (h o) -> p (c h) o", o=1).to_broadcast([128, 8, cs])
            else:
                kin = k_inv[:, ti, :, 0:n]
                kout = k_endT[:, :, 0:n]
                dbc = dec[:, ti, :, 0:1].to_broadcast([128, 4, n])
            P.op("dve", lambda e: e.tensor_tensor(out=kout, in0=kin, in1=dbc, op=ALU.mult),
                 ["kinv%d_%d" % (ti, c) for c in H4] + ["dec%d_%d" % (ti, c) for c in H4], ["kendT%d" % c for c in H4])

        def g_transposes(n, ti):
            H4 = [0, 1, 2, 3]
            pi, pk = ptr_alloc()
            for c in range(4):
                P.op("pe", lambda e, c=c: e.transpose(out=ptr[0:n, pi, c * 128:(c + 1) * 128], in_=k_endT[:, c, 0:n], identity=ident_b[:]), ["kendT%d" % c, "ident_b"], [pk],
                     signal=(c == 3))
            P.op("dve", lambda e: e.tensor_copy(out=k_end[0:n, ti, :], in_=ptr[0:n, pi, 0:512]), [pk], ["kend%d_%d" % (ti, c) for c in H4])

        def state_update(c, ch, ti, dSb, dSk, slot_cur):
            P.op("dve", lambda e: e.scalar_tensor_tensor(out=S[:, c, :], in0=S[:, c, :], scalar=dec[:, ti, c, ch:ch + 1], in1=pmm[:, dSb, c * 128:(c + 1) * 128],
                                                         op0=ALU.mult, op1=ALU.add), ["S%d" % c, "dec%d_%d" % (ti, c), dSk], ["S%d" % c])
            nx = 1 - slot_cur
            P.op(aux["eng"], lambda e: e.tensor_copy(out=S_bf[:, nx, c, :], in_=S[:, c, :]), ["S%d" % c], ["Sbf%d_%d" % (c, nx)])

        sbf_cur = [0, 0, 0, 0]

        def meta_prologue(sQF, sIG, sAB):
            xm = h[0:16, 4, :]
            dma(xm, metad, [], ["h4"], "xin4")
            front(xm, "h4", 16, lambda: hnT[:, 0, :, 0:16], "hnT0")
            nxt("hnT", 2)
            rhs = lambda k: hnT[:, 0, k, 0:16]
            vb, vk = proj_tm(sIG, 0, lambda k: hnT[:, 0, k, 0:16], 16, ["hnT0"])
            P.op("act", lambda e: e.activation(out=v_t[0:16, 0, :], in_=pmm[0:16, vb, :], func=AF.Copy), [vk], ["v0"])
            for c in range(4):
                fb, fk = proj_fm(sQF, 512, c, rhs, 16, ["hnT0"])
                f_chain(fb, fk, c, 16, 0, 1)
            g_wide(16, 0, 1, False)
            g_transposes(16, 0)
            db, dk = pm_alloc()
            for c in range(4):
                P.op("pe", lambda e, c=c: e.matmul(pmm[:, db, c * 128:(c + 1) * 128], lhsT=k_end[0:16, 0, c * 128:(c + 1) * 128],
                                                   rhs=v_t[0:16, 0, c * 128:(c + 1) * 128], start=True, stop=True),
                     ["kend0_%d" % c, "v0"], [dk], signal=(c == 3))
            for c in range(4):
                state_update(c, 0, 0, db, dk, sbf_cur[c])
                sbf_cur[c] = 1 - sbf_cur[c]
            for c in range(4):
                bb, bbk = proj_fm(sAB, 512, c, rhs, 16, ["hnT0"])
                r = nxt("th3", 2)
                P.op("act", lambda e, bb=bb, r=r: e.activation(out=r_th3[:, r, 0:16], in_=pmm[:, bb, 0:16], func=AF.Tanh, scale=0.5), [bbk], ["th3_%d" % r])
                ab, abk = proj_fm(sAB, 0, c, rhs, 16, ["hnT0"])
                P.op("dve", lambda e, ab=ab, r=r, c=c: e.scalar_tensor_tensor(out=convbuf[:, c, 14:30], in0=r_th3[:, r, 0:16], scalar=1.0, in1=pmm[:, ab, 0:16],
                                                                             op0=ALU.add, op1=ALU.mult), ["th3_%d" % r, abk], ["cb%d" % c])

        def tile_parts(seg, j, slots, last_tile):
            t0 = seg * 512 + j * 128
            hk = "h%d" % j
            cx = {}

            def A0a():
                if "xi" in cx:
                    return
                dma(h[:, j, :], xp[t0:t0 + 128, :], [], [hk], "xin%d" % j)
                cx["xi"] = front_a(h[:, j, :], hk, 128)

            def A0b():
                if "hi" in cx:
                    return
                cx["hi"] = nxt("hnT", 2)
                cx["ti"] = nxt("tile", 2)
                hi = cx["hi"]
                front_b(cx["xi"], 128, lambda: hnT[:, hi, :, :], "hnT%d" % hi)

            def rhs(k):
                return hnT[:, cx["hi"], k, :]

            def A1():
                for c in range(4):
                    fb, fk = proj_fm(slots["QF"], 512, c, rhs, 128, ["hnT%d" % cx["hi"]])
                    f_chain(fb, fk, c, 128, cx["ti"], 2)

            def A2():
                for c in range(4):
                    qb, qk = proj_fm(slots["QF"], 0, c, rhs, 128, ["hnT%d" % cx["hi"]])
                    P.op("act", lambda e, qb=qb, c=c: e.activation(out=r_sq[:, c, :], in_=pmm[:, qb, 0:128], func=AF.Silu), [qk], ["sq%d" % c])

            def gAa():
                g_wide_a(128)

            def gAb():
                g_wide_b(128, cx["ti"], 2, True)

            def gB():
                g_transposes(128, cx["ti"])

            def A3():
                hkx = "hnT%d" % cx["hi"]
                for c in range(4):
                    bb, bbk = proj_fm(slots["AB"], 512, c, rhs, 128, [hkx])
                    r = nxt("th3", 2)
                    P.op("act", lambda e, bb=bb, r=r: e.activation(out=r_th3[:, r, :], in_=pmm[:, bb, 0:128], func=AF.Tanh, scale=0.5), [bbk], ["th3_%d" % r])
                    ab, abk = proj_fm(slots["AB"], 0, c, rhs, 128, [hkx])
                    P.op("dve", lambda e, ab=ab, r=r, c=c: e.scalar_tensor_tensor(out=convbuf[:, c, 30 + j * 128:30 + (j + 1) * 128], in0=r_th3[:, r, :], scalar=1.0,
                                                                                 in1=pmm[:, ab, 0:128], op0=ALU.add, op1=ALU.mult), ["th3_%d" % r, abk], ["cb%d" % c])
                    if last_tile:
                        P.op("dve", lambda e, ab=ab, r=r, c=c: e.scalar_tensor_tensor(out=glu_last[:, c, 0:30], in0=r_th3[:, r, 98:128], scalar=1.0,
                                                                                     in1=pmm[:, ab, 98:128], op0=ALU.add, op1=ALU.mult), ["th3_%d" % r, abk], ["glu_last"])

            def A4():
                hi, ti = cx["hi"], cx["ti"]
                hkx = "hnT%d" % hi
                vb, vk = proj_tm(slots["IG"], 0, lambda k: hnT[:, hi, k, :], 128, [hkx])
                P.op("dve", lambda e: e.tensor_copy(out=v_t[:, ti, :], in_=pmm[:, vb, :]), [vk], ["v%d" % ti])
                gb, gk = proj_tm(slots["IG"], 512, lambda k: hnT[:, hi, k, :], 128, [hkx])
                P.op("act", lambda e: e.activation(out=gate[:, ti, :], in_=pmm[:, gb, :], func=AF.Silu), [gk], ["ovlB%d" % ti])

            SB, OB = 0, 1
            sk_, ok_ = "pmm0", "pmm1"

            def B0():
                ti = cx["ti"]
                for c in range(4):
                    P.op("pe", lambda e, c=c: e.matmul(pmm[:, SB, c * 128:(c + 1) * 128], lhsT=k_inv[:, ti, c, :], rhs=q_dec[:, ti, c, :], start=True, stop=True),
                         ["kinv%d_%d" % (ti, c), "qdec%d_%d" % (ti, c)], [sk_], signal=(c == 3))
                for c in range(4):
                    P.op("dve", lambda e, c=c: e.tensor_tensor(out=scm[:, c, :], in0=pmm[:, SB, c * 128:(c + 1) * 128], in1=mask2[:], op=ALU.mult),
                         [sk_, "mask2"], ["scm%d" % c])

            def B1():
                ti = cx["ti"]
                for c in range(4):
                    P.op("pe", lambda e, c=c: e.matmul(pmm[:, OB, c * 128:(c + 1) * 128], lhsT=scm[:, c, :], rhs=v_t[:, ti, c * 128:(c + 1) * 128], start=(c == 0), stop=(c == 0),
                                                       skip_group_check=(c != 0)),
                         ["scm%d" % c, "v%d" % ti], [ok_], signal=False)

            def Bch(ch):
                ti = cx["ti"]
                lo, hi_ = ch * 64, (ch + 1) * 64
                for c in range(4):
                    cur = sbf_cur[c]
                    P.op("pe", lambda e, c=c, cur=cur: e.matmul(pmm[lo:hi_, OB, c * 128:(c + 1) * 128], lhsT=q_dec[:, ti, c, lo:hi_], rhs=S_bf[:, cur, c, :],
                                                                start=False, stop=True, skip_group_check=True),
                         ["qdec%d_%d" % (ti, c), "Sbf%d_%d" % (c, cur)], [ok_], signal=(ch == 1 and c == 3))
                for c in range(4):
                    P.op("pe", lambda e, c=c: e.matmul(pmm[:, SB, c * 128:(c + 1) * 128], lhsT=k_end[lo:hi_, ti, c * 128:(c + 1) * 128],
                                                       rhs=v_t[lo:hi_, ti, c * 128:(c + 1) * 128], start=True, stop=True),
                         ["kend%d_%d" % (ti, c), "v%d" % ti], [sk_], signal=(c == 3))
                for c in range(4):
                    state_update(c, ch, ti, SB, sk_, sbf_cur[c])
                    sbf_cur[c] = 1 - sbf_cur[c]

            def B4a():
                ti = cx["ti"]
                yi = nxt("yhg", 2)
                cx["yi"] = yi
                ss, ssk = stat_alloc(4)
                for c in range(4):
                    P.op("act", lambda e, c=c: e.activation(out=y_hg[:, yi, c * 128:(c + 1) * 128], in_=pmm[:, OB, c * 128:(c + 1) * 128], func=AF.Square,
                                                            accum_out=ss[:, c:c + 1]), [ok_], ["yhg%d" % yi, ssk])
                rs, rsk = rstd_from_ss(ss, ssk, 128, 1.0 / 128)
                for c in range(4):
                    P.op("dve", lambda e, c=c: e.scalar_tensor_tensor(out=y_hg[:, yi, c * 128:(c + 1) * 128], in0=pmm[:, OB, c * 128:(c + 1) * 128], scalar=rs[:, c:c + 1],
                                                                      in1=gate[:, ti, c * 128:(c + 1) * 128], op0=ALU.mult, op1=ALU.mult),
                         [ok_, rsk, "ovlB%d" % ti], ["yhg%d" % yi])

            def B4b():
                yi = cx["yi"]
                pi, pk = ptr_alloc()
                pv = ptr[:, pi, 0:512].rearrange("p (k t) -> p k t", k=4)
                for c in range(4):
                    P.op("pe", lambda e, c=c: e.transpose(out=pv[:, c, :], in_=y_hg[:, yi, c * 128:(c + 1) * 128], identity=ident_b[:]), ["yhg%d" % yi, "ident_b"], [pk], signal=(c == 3))
                P.op("act", lambda e: e.activation(out=mixT[:, 0:4, j * 128:(j + 1) * 128], in_=pv, func=AF.Copy), [pk], ["mixhg%d" % j])

            return dict(A0a=A0a, A0b=A0b, A1=A1, A2=A2, A3=A3, A4=A4, gAa=gAa, gAb=gAb, gB=gB, B0=B0, B1=B1, B2=lambda: Bch(0), B3=lambda: Bch(1), B4a=B4a, B4b=B4b)

        def run_step(B, A, F, prevB, filler, F3=None):
            fl = list(filler or [])

            def fill():
                if fl:
                    fl.pop(0)()
            if A:
                A["A1"]()
                A["A2"]()
            fill()
            if A: A["A3"]()
            if B: B["B0"]()
            if A: A["A4"]()
            fill()
            if B:
                B["B1"]()
                B["gB"]()
                B["B2"]()
            fill()
            if prevB: prevB["B4b"]()
            if F: F["A0b"]()
            fill()
            if B: B["B3"]()
            if A: A["gAa"]()
            if F3: F3["A0a"]()
            if B: B["B4a"]()
            if A: A["gAb"]()

        DWK = ["ovl0a", "ovl0b"]

        def conv_chunks():
            def mk(half, c):
                hs0 = half * 256

                def f():
                    bi, bk = pm_alloc()
                    for jt in range(31):
                        P.op("pe", lambda e, jt=jt, b=bi: e.matmul(pmm[:, b, 0:256], lhsT=diag[:, c, jt, :], rhs=convbuf[:, c, hs0 + jt:hs0 + jt + 256], start=(jt == 0), stop=(jt == 30)),
                             ["diag", "cb%d" % c], [bk], signal=(jt == 30))
                    P.op("act", lambda e, b=bi: e.activation(out=dw[:, c, hs0:hs0 + 256], in_=pmm[:, b, 0:256], func=AF.Identity, bias=cols[:, 25 + c:26 + c]), [bk, "cols"], [DWK[half]])
                    if half == 1:
                        P.op("dve", lambda e: e.tensor_copy(out=convbuf[:, c, 0:30], in_=convbuf[:, c, 512:542]), ["cb%d" % c], ["cb%d" % c])
                return f
            return [mk(hf, c) for hf in range(2) for c in range(4)]

        def ln_half(half):
            hs = slice(half * 256, (half + 1) * 256)
            dk = DWK[half]
            ab, ak = pm_alloc()
            for c in range(4):
                P.op("pe", lambda e, c=c, b=ab: e.matmul(pmm[:, b, 0:256], lhsT=ones_f[:], rhs=dw[:, c, hs], start=(c == 0), stop=(c == 3)), ["ones_f", dk], [ak], signal=(c == 3))
            qb_, qk_ = pm_alloc()
            for c in range(4):
                r = nxt("dwsq", 2)
                P.op("act", lambda e, c=c, r=r: e.activation(out=dwsq[:, r, 0:256], in_=dw[:, c, hs], func=AF.Square), [dk], ["ovl1"])
                P.op("pe", lambda e, c=c, r=r, b=qb_: e.matmul(pmm[:, b, 0:256], lhsT=ones_f[:], rhs=dwsq[:, r, 0:256], start=(c == 0), stop=(c == 3)), ["ones_f", "ovl1"], [qk_],
                     signal=True)
            mean, msq, rstd, nmr = ln_t[:, 0, :], ln_t[:, 1, :], ln_t[:, 2, :], ln_t[:, 3, :]
            P.op("dve", lambda e, b=ab: e.tensor_scalar(out=mean, in0=pmm[:, b, 0:256], scalar1=1.0 / 512, scalar2=None, op0=ALU.mult), [ak], ["ovlB0"])
            P.op("dve", lambda e: e.tensor_tensor(out=msq, in0=mean, in1=mean, op=ALU.mult), ["ovlB0"], ["ovlB0"])
            P.op("dve", lambda e, b=qb_: e.scalar_tensor_tensor(out=msq, in0=pmm[:, b, 0:256], scalar=1.0 / 512, in1=msq, op0=ALU.mult, op1=ALU.subtract), [qk_, "ovlB0"], ["ovlB0"])
            P.op("act", lambda e: e.activation(out=rstd, in_=msq, func=AF.Ln, bias=EPS_AP), ["ovlB0", "cols1"], ["ovlB1"])
            P.op("act", lambda e: e.activation(out=rstd, in_=rstd, func=AF.Exp, scale=-0.5), ["ovlB1"], ["ovlB1"])
            P.op("dve", lambda e: e.tensor_tensor(out=nmr, in0=mean, in1=rstd, op=ALU.mult), ["ovlB0", "ovlB1"], ["ovlB1"])
            rb = ln_t[:, 2:3, :].to_broadcast([128, 4, 256])
            nb_ = ln_t[:, 3:4, :].to_broadcast([128, 4, 256])
            P.op("dve", lambda e: e.tensor_tensor(out=dw[:, :, hs], in0=dw[:, :, hs], in1=rb, op=ALU.mult), [dk, "ovlB1"], [dk])
            P.op("dve", lambda e: e.tensor_tensor(out=dw[:, :, hs], in0=dw[:, :, hs], in1=nb_, op=ALU.subtract), [dk, "ovlB1"], [dk])
            for c in range(4):
                P.op("act", lambda e, c=c: e.activation(out=mixT[:, 4 + c, hs], in_=dw[:, c, hs], func=AF.Silu, scale=cols[:, 29 + c:30 + c], bias=cols[:, 33 + c:34 + c]),
                     [dk, "cols"], ["mixcv%d" % half])

        def seg_1bc(seg, sWO, last_seg, conv_rest, extra=None):
            ex = list(extra or [])

            def exrun():
                if ex:
                    ex.pop(0)()
            if last_seg:
                bi, bk = pm_alloc()
                for c in range(4):
                    P.op("pe", lambda e, c=c, b=bi: e.transpose(out=pmm[0:30, b, c * 128:(c + 1) * 128], in_=glu_last[:, c, 0:30], identity=ident_f[:]),
                         ["glu_last", "ident_f"], [bk], signal=(c == 3))
                P.op("act", lambda e, b=bi: e.activation(out=gl_out[0:30, :], in_=pmm[0:30, b, :], func=AF.Copy, scale=0.5), [bk], ["gl_out"])
                dma(scp, gl_out[0:30, :], ["gl_out"], [], "o_scp", is_out=True)
            n1 = min(4, len(conv_rest))
            for f in conv_rest[:len(conv_rest) - n1]:
                f()
            exrun()
            ln_half(0)
            exrun()
            for f in conv_rest[len(conv_rest) - n1:]:
                f()
            exrun()
            ln_half(1)
            pend = []
            for j in range(4):
                pend.append(wout_tile(j, 128, sWO, ["mixhg%d" % j, "mixcv%d" % (j // 2)], defer=True))
                if j >= 1:
                    pend.pop(0)()
            pend.pop(0)()
            while ex:
                ex.pop(0)()

        def wout_tile(j, n, sWO, mkeys, defer=False):
            cs = slice(j * 128, j * 128 + n)
            bi, bks = pm_pair()
            for nn in range(2):
                for kc in range(8):
                    P.op("pe", lambda e, nn=nn, kc=kc: e.matmul(pmm[0:n, bi + nn, :], lhsT=mixT[:, kc, cs], rhs=arena[:, sWO, kc, nn * 512:(nn + 1) * 512],
                                                             start=(kc == 0), stop=(kc == 7)),
                         mkeys + ["slot%dk%d" % (sWO, kc)], [bks[nn]], signal=(kc == 7))
            hk = "h%d" % j
            for nn in range(2):
                P.op("dve", lambda e, nn=nn: e.tensor_tensor(out=h[0:n, j, nn * 512:(nn + 1) * 512], in0=h[0:n, j, nn * 512:(nn + 1) * 512], in1=pmm[0:n, bi + nn, :], op=ALU.add),
                     [hk, bks[nn]], [hk])
            xi = front_a(h[0:n, j, :], hk, n)
            fb = lambda: front_b(xi, n, lambda: hn2T[:, :, cs], "hn2T%d" % j)
            if defer:
                return fb
            fb()

        def mlp_quarter(seg, q, sU, sD, ntile, nsmp, hooks=None, smp_ops=None):
            ncol = 512 + nsmp
            ui = nxt("uT", 2)
            uks = ["ovl0a", "ovl0b"] if ui == 0 else ["ovl1"]
            hkeys = ["hn2T%d" % j for j in range(ntile)]
            for fc in range(8):
                bi, bk = pm_alloc()
                for k in range(8):
                    P.op("pe", lambda e, k=k, fc=fc, b=bi: e.matmul(pmm[:, b, :], lhsT=arena[:, sU, k, fc * 128:(fc + 1) * 128], rhs=hn2T[:, k, 0:512], start=(k == 0), stop=(k == 7)),
                         ["slot%dk%d" % (sU, k)] + hkeys[0:4], [bk], signal=(k == 7))
                r = nxt("rbuf", 2)
                P.op("act", lambda e, b=bi, r=r: e.activation(out=rbuf[:, r, :], in_=pmm[:, b, :], func=AF.Relu), [bk], ["ovlB%d" % r])
                P.op("dve", lambda e, fc=fc, r=r: e.tensor_tensor(out=uT[ui][:, fc, :], in0=rbuf[:, r, :], in1=rbuf[:, r, :], op=ALU.mult), ["ovlB%d" % r], uks)
            if nsmp:
                usi = nxt("uTs", 2)
                bi, bk = pm_alloc()
                for fc in range(8):
                    for k in range(8):
                        P.op("pe", lambda e, k=k, fc=fc, bi=bi: e.matmul(pmm[:, bi, fc * 16:(fc + 1) * 16], lhsT=arena[:, sU, k, fc * 128:(fc + 1) * 128], rhs=hn2T[:, k, 512:528],
                                                                         start=(k == 0), stop=(k == 7)),
                             ["slot%dk%d" % (sU, k), "hn2T4"], [bk], signal=(fc == 7 and k == 7))
                P.op("act", lambda e, bi=bi: e.activation(out=sm[:, 12, :], in_=pmm[:, bi, 0:64], func=AF.Relu), [bk], ["s_rs"])
                P.op("act", lambda e, bi=bi: e.activation(out=sm[:, 13, :], in_=pmm[:, bi, 64:128], func=AF.Relu), [bk], ["s_tmp"])
                P.op("dve", lambda e: e.tensor_tensor(out=uTs[:, usi, 0:4, :], in0=v4(sm[:, 12, :]), in1=v4(sm[:, 12, :]), op=ALU.mult), ["s_rs"], ["uTs%d" % usi])
                P.op("dve", lambda e: e.tensor_tensor(out=uTs[:, usi, 4:8, :], in0=v4(sm[:, 13, :]), in1=v4(sm[:, 13, :]), op=ALU.mult), ["s_tmp"], ["uTs%d" % usi])
                bi, bks = pm_pair()
                for nn in range(2):
                    for fc in range(8):
                        P.op("pe", lambda e, nn=nn, fc=fc, bi=bi: e.matmul(pmm[0:16, bi + nn, :], lhsT=uTs[:, usi, fc, :], rhs=arena[:, sD, fc, nn * 512:(nn + 1) * 512],
                                                                          start=(fc == 0), stop=(fc == 7)),
                             ["uTs%d" % usi, "slot%dk%d" % (sD, fc)], [bks[nn]], signal=(fc == 7))
                for nn in range(2):
                    P.op("dve", lambda e, nn=nn, bi=bi: e.tensor_tensor(out=h[0:16, 4, nn * 512:(nn + 1) * 512], in0=h[0:16, 4, nn * 512:(nn + 1) * 512], in1=pmm[0:16, bi + nn, :], op=ALU.add),
                         ["h4", bks[nn]], ["h4"])
                if q == 3:
                    final_tile(h[0:16, 4, :], "h4", 16, ysd, "o_ys")
            if hooks and "after_up" in hooks:
                hooks["after_up"]()
            for j in range(4):
                bi, bks = pm_pair()
                for nn in range(2):
                    for fc in range(8):
                        P.op("pe", lambda e, nn=nn, fc=fc, j=j, bi=bi: e.matmul(pmm[:, bi + nn, :], lhsT=uT[ui][:, fc, j * 128:(j + 1) * 128], rhs=arena[:, sD, fc, nn * 512:(nn + 1) * 512],
                                                                      start=(fc == 0), stop=(fc == 7)),
                             uks + ["slot%dk%d" % (sD, fc)], [bks[nn]], signal=(fc == 7))
                hk = "h%d" % j
                for nn in range(2):
                    P.op("dve", lambda e, nn=nn, j=j, bi=bi: e.tensor_tensor(out=h[:, j, nn * 512:(nn + 1) * 512], in0=h[:, j, nn * 512:(nn + 1) * 512], in1=pmm[:, bi + nn, :], op=ALU.add),
                         [hk, bks[nn]], [hk])
                if q == 3:
                    final_tile(h[:, j, :], hk, 128, yp[seg * 512 + j * 128:seg * 512 + (j + 1) * 128, :], "o_yp%d" % j)
                    if hooks and j in hooks:
                        hooks[j]()
                if smp_ops:
                    smp_ops()

        def final_tile(hap, hk, n, dst, ringname):
            ss, ssk = stat_alloc()
            junk = y_hg[:].rearrange("p r f -> p (r f)")
            P.op("act", lambda e: e.activation(out=junk[0:n, :], in_=hap, func=AF.Square, accum_out=ss[0:n, :]), [hk], ["yhg0", "yhg1", ssk])
            rs, rsk = rstd_from_ss(ss[0:n, :], ssk, n, 1.0 / 1024)
            P.op("dve", lambda e: e.scalar_tensor_tensor(out=hap, in0=hap, scalar=rs[0:n, :], in1=fg_bc[0:n, :], op0=ALU.mult, op1=ALU.mult), [hk, rsk, "fg_bc"], [hk])
            dma(dst, hap, [hk], [], ringname, is_out=True)


        Sin = [ovlA[:, i * 512:(i + 1) * 512].rearrange("p (h v) -> p h v", h=4) for i in range(3)]
        Xc = [ovlA[0:30, 1536 + i * 512:1536 + (i + 1) * 512] for i in range(2)]
        vmk = ovlA[0:16, 2560:3072]
        k_tok = ovlA[0:16, 3072:3584]
        v_tok = ovlA[0:16, 3584:4096]
        glu_tok = gl_out[0:16, :]
        SMK = ["s_Sin0", "s_Sin1", "s_Sin2", "s_X0", "s_X1", "s_vm", "s_ktok", "s_vtok"]
        (QT, TH2, FT, KT, VT, GT, TH3, GLU, OT, DWT, DWS, SQ, RS, TMP) = [sm[:, i, :] for i in range(14)]
        v4 = lambda ap: ap.rearrange("p (c t) -> p c t", c=4)

        def sample_part1(sQF, sIG, sAB):
            dma(h[0:16, 4, :], xsd, [], ["h4"], "xin4")
            front(h[0:16, 4, :], "h4", 16, lambda: hnTs[:, :, :], "hnTs")
            zb, zk = pm_alloc()
            for g in range(24):
                slot = (sQF, sIG, sAB)[g // 8]
                co = (g % 8) * 128
                for k in range(8):
                    P.op("pe", lambda e, g=g, k=k, slot=slot, co=co: e.matmul(pmm[:, zb, g * 16:(g + 1) * 16], lhsT=arena[:, slot, k, co:co + 128], rhs=hnTs[:, k, :],
                                                                           start=(k == 0), stop=(k == 7)),
                         ["slot%dk%d" % (slot, k), "hnTs"], [zk], signal=(g == 23 and k == 7))
            z = lambda i: pmm[:, zb, i * 64:(i + 1) * 64]
            P.op("act", lambda e: e.activation(out=QT, in_=z(0), func=AF.Silu), [zk], ["s_qt"])
            P.op("act", lambda e: e.activation(out=GT, in_=z(3), func=AF.Silu), [zk], ["s_gt"])
            P.op("act", lambda e: e.activation(out=TH2, in_=z(1), func=AF.Tanh, scale=-0.5), [zk], ["s_th2"])
            P.op("act", lambda e: e.activation(out=TH3, in_=z(5), func=AF.Tanh, scale=0.5), [zk], ["s_th3"])
            P.op("act", lambda e: e.activation(out=VT, in_=z(2), func=AF.Copy), [zk], ["s_vt"])
            for c in range(4):
                P.op("dve", lambda e, c=c: e.tensor_scalar(out=v4(FT)[:, c, :], in0=v4(TH2)[:, c, :], scalar1=C1(c), scalar2=C2(c), op0=ALU.mult, op1=ALU.add),
                     ["s_th2"] + DCK, ["s_ft"])
                P.op("dve", lambda e, c=c: e.tensor_scalar(out=v4(KT)[:, c, :], in0=v4(TH2)[:, c, :], scalar1=HOML(c), scalar2=HOML(c), op0=ALU.mult, op1=ALU.add),
                     ["s_th2"] + DCK, ["s_kt"])
            P.op("dve", lambda e: e.scalar_tensor_tensor(out=GLU, in0=TH3, scalar=1.0, in1=z(4), op0=ALU.add, op1=ALU.mult), ["s_th3", zk], ["s_glu"])
            P.op("dve", lambda e: e.tensor_scalar(out=GLU, in0=GLU, scalar1=0.5, scalar2=None, op0=ALU.mult), ["s_glu"], ["s_glu"])
            for src, skey, dst, dkey in ((GLU, "s_glu", glu_tok, "gl_out"),):
                bi, bk = pm_alloc()
                for c in range(4):
                    P.op("pe", lambda e, c=c, bi=bi, src=src: e.transpose(out=pmm[0:16, bi, c * 128:(c + 1) * 128], in_=v4(src)[:, c, :], identity=ident_f[:]),
                         [skey, "ident_f"], [bk], signal=(c == 3))
                P.op("act", lambda e, bi=bi, dst=dst: e.activation(out=dst, in_=pmm[0:16, bi, :], func=AF.Copy), [bk], [dkey])
            dma(scs[:, 0:29, :], scd[:, 1:30, :], [], [], "o_scs0", is_out=True)
            dma(scs[:, 29, :], glu_tok, ["gl_out"], [], "o_scs1", is_out=True)

        H4_ = [0, 1, 2, 3]
        SB1 = dict(Sin=Sin, SinK=[["s_Sin%d" % i] for i in range(3)], X=Xc, XK=[["s_X0"], ["s_X1"]], vmk=vmk, vmkK=["s_vm"],
                   ktok=k_tok, ktokK=["s_ktok"], vtok=v_tok, vtokK=["s_vtok"], tag="a")
        SB2 = dict(Sin=[LF[:], BB[:], EN[:]], SinK=[["lf"], ["bb"], ["en"]],
                   X=[r_sq[:].rearrange("p c t -> p (c t)")[0:30, :], r_th2[:].rearrange("p c t -> p (c t)")[0:30, :]],
                   XK=[["sq%d" % c for c in H4_], ["th2_%d" % c for c in H4_]],
                   vmk=k_end[:].rearrange("p r f -> p (r f)").bitcast(F32)[0:16, :], vmkK=["kend%d_%d" % (t, c) for t in range(2) for c in H4_],
                   ktok=v_t[:].rearrange("p r f -> p (r f)").bitcast(F32)[0:16, :], ktokK=["v0", "v1"],
                   vtok=q_dec[:].rearrange("p r c t -> p (r c t)").bitcast(F32)[0:16, :], vtokK=["qdec%d_%d" % (t, c) for t in range(2) for c in H4_], tag="b")

        def sample_tok(SB):
            for src, skey, dst, dkeys in ((KT, "s_kt", SB["ktok"], SB["ktokK"]), (VT, "s_vt", SB["vtok"], SB["vtokK"])):
                bi, bk = pm_alloc()
                for c in range(4):
                    P.op("pe", lambda e, c=c, bi=bi, src=src: e.transpose(out=pmm[0:16, bi, c * 128:(c + 1) * 128], in_=v4(src)[:, c, :], identity=ident_f[:]),
                         [skey, "ident_f"], [bk], signal=(c == 3))
                P.op("act", lambda e, bi=bi, dst=dst: e.activation(out=dst, in_=pmm[0:16, bi, :], func=AF.Copy), [bk], dkeys)

        def sample_load(SB, b):
            t = SB["tag"]
            dma(SB["Sin"][b % 3], shd[b].rearrange("h d v -> d h v"), [], SB["SinK"][b % 3], "s_in%s%d" % (t, b % 3))
            dma(SB["X"][b % 2], scd[b], [], SB["XK"][b % 2], "s_x%s%d" % (t, b % 2))

        def sample_b(SB, b, first, last):
            i = b % 3
            xi = b % 2
            t = SB["tag"]
            Si, SiK, Xi, XiK = SB["Sin"][i], SB["SinK"][i], SB["X"][xi], SB["XK"][xi]
            vm, kt, vt = SB["vmk"], SB["ktok"], SB["vtok"]
            if first:
                sample_load(SB, b)
            if not last:
                sample_load(SB, b + 1)
            P.op("dve", lambda e: e.tensor_scalar(out=vm, in0=vt, scalar1=ident_f[0:16, b:b + 1], scalar2=None, op0=ALU.mult), SB["vtokK"] + ["ident_f"], SB["vmkK"])
            kb, kk = pm_alloc()
            for hh in range(4):
                P.op("pe", lambda e, hh=hh: e.matmul(pmm[:, kb, hh * 128:(hh + 1) * 128], lhsT=kt[:, hh * 128:(hh + 1) * 128], rhs=vm[:, hh * 128:(hh + 1) * 128],
                                                     start=True, stop=True), SB["ktokK"] + SB["vmkK"], [kk], signal=(hh == 3))
            for hh in range(4):
                P.op("dve", lambda e, hh=hh: e.scalar_tensor_tensor(out=Si[:, hh, :], in0=Si[:, hh, :], scalar=v4(FT)[:, hh, b:b + 1], in1=pmm[:, kb, hh * 128:(hh + 1) * 128],
                                                                    op0=ALU.mult, op1=ALU.add), ["s_ft", kk], SiK)
            dma(shs[b].rearrange("h d v -> d h v"), Si, SiK, [], "s_out%s%d" % (t, i), is_out=True)
            ob_, ok2 = pm_alloc()
            for hh in range(4):
                P.op("pe", lambda e, hh=hh: e.matmul(pmm[:, ob_, hh:hh + 1], lhsT=Si[:, hh, :], rhs=v4(QT)[:, hh, b:b + 1], start=True, stop=True),
                     SiK + ["s_qt"], [ok2], signal=(hh == 3))
            P.op("act", lambda e: e.activation(out=v4(OT)[:, :, b], in_=pmm[:, ob_, 0:4], func=AF.Copy), [ok2], ["s_ot"])
            P.op("dve", lambda e: e.tensor_tensor(out=Xi, in0=Xi, in1=cwrows[0:30, :], op=ALU.mult), ["cwrows"], XiK)
            db_, dk2 = pm_alloc()
            for c in range(4):
                P.op("pe", lambda e, c=c: e.matmul(pmm[:, db_, c:c + 1], lhsT=Xi[:, c * 128:(c + 1) * 128], rhs=ones_f[0:30, 0:1], start=True, stop=True),
                     XiK + ["ones_f"], [dk2], signal=(c == 3))
            P.op("act", lambda e: e.activation(out=v4(DWT)[:, :, b], in_=pmm[:, db_, 0:4], func=AF.Copy), [dk2], ["s_dwt"])

        def smp_A(SB, b, nxt_b):
            i, xi, t = b % 3, b % 2, SB["tag"]
            dma(SB["Sin"][i], shd[b].rearrange("h d v -> d h v"), [], SB["SinK"][i], "s_in%s%d" % (t, i))
            P.op(aux["eng"], lambda e: e.tensor_scalar(out=SB["vmk"], in0=SB["vtok"], scalar1=ident_f[0:16, b:b + 1], scalar2=1.0, op0=ALU.mult, op1=ALU.mult),
                 SB["vtokK"] + ["ident_f"], SB["vmkK"])
            Xi, XiK = SB["X"][xi], SB["XK"][xi]
            P.op(aux["eng"], lambda e: e.tensor_tensor(out=Xi, in0=Xi, in1=cwrows[0:30, :], op=ALU.mult), ["cwrows"], XiK)
            if nxt_b is not None:
                dma(SB["X"][nxt_b % 2], scd[nxt_b], [], SB["XK"][nxt_b % 2], "s_x%s%d" % (t, nxt_b % 2))

        def smp_B(SB, b):
            i, xi, t = b % 3, b % 2, SB["tag"]
            Si, SiK, Xi, XiK = SB["Sin"][i], SB["SinK"][i], SB["X"][xi], SB["XK"][xi]
            vm, kt = SB["vmk"], SB["ktok"]
            kb, kk = pm_alloc()
            for hh in range(4):
                P.op("pe", lambda e, hh=hh: e.matmul(pmm[:, kb, hh * 128:(hh + 1) * 128], lhsT=kt[:, hh * 128:(hh + 1) * 128], rhs=vm[:, hh * 128:(hh + 1) * 128],
                                                     start=True, stop=True), SB["ktokK"] + SB["vmkK"], [kk], signal=(hh == 3))
            for hh in range(4):
                P.op("dve", lambda e, hh=hh: e.scalar_tensor_tensor(out=Si[:, hh, :], in0=Si[:, hh, :], scalar=v4(FT)[:, hh, b:b + 1], in1=pmm[:, kb, hh * 128:(hh + 1) * 128],
                                                                    op0=ALU.mult, op1=ALU.add), ["s_ft", kk], SiK)
            dma(shs[b].rearrange("h d v -> d h v"), Si, SiK, [], "s_out%s%d" % (t, i), is_out=True)
            db_, dk2 = pm_alloc()
            for c in range(4):
                P.op("pe", lambda e, c=c: e.matmul(pmm[:, db_, c:c + 1], lhsT=Xi[:, c * 128:(c + 1) * 128], rhs=ones_f[0:30, 0:1], start=True, stop=True),
                     XiK + ["ones_f"], [dk2], signal=(c == 3))
            P.op("act", lambda e: e.activation(out=v4(DWT)[:, :, b], in_=pmm[:, db_, 0:4], func=AF.Copy), [dk2], ["s_dwt"])

        def smp_D(SB, b):
            i = b % 3
            Si, SiK = SB["Sin"][i], SB["SinK"][i]
            ob_, ok2 = pm_alloc()
            for hh in range(4):
                P.op("pe", lambda e, hh=hh: e.matmul(pmm[:, ob_, hh:hh + 1], lhsT=Si[:, hh, :], rhs=v4(QT)[:, hh, b:b + 1], start=True, stop=True),
                     SiK + ["s_qt"], [ok2], signal=(hh == 3))
            P.op("act", lambda e: e.activation(out=v4(OT)[:, :, b], in_=pmm[:, ob_, 0:4], func=AF.Copy), [ok2], ["s_ot"])

        def sample_pipe(SB, bs):
            st = {"n": 0}
            t = SB["tag"]

            def step():
                n = st["n"]
                st["n"] += 1
                if n == 0 and bs:
                    dma(SB["X"][bs[0] % 2], scd[bs[0]], [], SB["XK"][bs[0] % 2], "s_x%s%d" % (t, bs[0] % 2))
                if 0 <= n - 2 < len(bs):
                    smp_D(SB, bs[n - 2])
                if 0 <= n - 1 < len(bs):
                    smp_B(SB, bs[n - 1])
                if n < len(bs):
                    smp_A(SB, bs[n], bs[n + 1] if n + 1 < len(bs) else None)
                return n + 1 >= len(bs) + 2
            return step

        def sample_part2_stages(sWO):
          def S1():
            P.op("act", lambda e: e.activation(out=SQ, in_=OT, func=AF.Square), ["s_ot"], ["s_sq"])

          def S2():
            bi, bk = pm_alloc()
            P.op("pe", lambda e: e.matmul(pmm[:, bi, 0:64], lhsT=ones_f[:], rhs=SQ, start=True, stop=True), ["ones_f", "s_sq"], [bk])
            P.op("act", lambda e: e.activation(out=RS, in_=pmm[:, bi, 0:64], func=AF.Ln, scale=1.0 / 128, bias=EPS_AP), [bk, "cols1"], ["s_rs"])
            P.op("act", lambda e: e.activation(out=RS, in_=RS, func=AF.Exp, scale=-0.5), ["s_rs"], ["s_rs"])
            P.op("dve", lambda e: e.tensor_tensor(out=TMP, in0=OT, in1=RS, op=ALU.mult), ["s_ot", "s_rs"], ["s_tmp"])
            P.op("dve", lambda e: e.tensor_tensor(out=mixT[:, 0:4, 512:528], in0=v4(TMP), in1=v4(GT), op=ALU.mult), ["s_tmp", "s_gt"], ["mixhg4"])
            for c in range(4):
                P.op("dve", lambda e, c=c: e.scalar_tensor_tensor(out=v4(DWS)[:, c, :], in0=v4(GLU)[:, c, :], scalar=cwT[:, c, 30:31], in1=v4(DWT)[:, c, :], op0=ALU.mult, op1=ALU.add),
                     ["s_glu", "cwT", "s_dwt"], ["s_dws"])
                P.op("dve", lambda e, c=c: e.tensor_scalar(out=v4(DWS)[:, c, :], in0=v4(DWS)[:, c, :], scalar1=cols[:, 25 + c:26 + c], scalar2=None, op0=ALU.add), ["s_dws", "cols"], ["s_dws"])
            P.op("act", lambda e: e.activation(out=SQ, in_=DWS, func=AF.Square), ["s_dws"], ["s_sq"])

          def S3():
            ab, ak = pm_alloc()
            for c in range(4):
                P.op("pe", lambda e, c=c: e.matmul(pmm[:, ab, 0:16], lhsT=ones_f[:], rhs=v4(DWS)[:, c, :], start=(c == 0), stop=(c == 3)), ["ones_f", "s_dws"], [ak], signal=(c == 3))
            qb2, qk2 = pm_alloc()
            for c in range(4):
                P.op("pe", lambda e, c=c: e.matmul(pmm[:, qb2, 0:16], lhsT=ones_f[:], rhs=v4(SQ)[:, c, :], start=(c == 0), stop=(c == 3)), ["ones_f", "s_sq"], [qk2], signal=(c == 3))
            mean, msq, rstd, nmr = ln_t[:, 0, 0:16], ln_t[:, 1, 0:16], ln_t[:, 2, 0:16], ln_t[:, 3, 0:16]
            P.op("dve", lambda e: e.tensor_scalar(out=mean, in0=pmm[:, ab, 0:16], scalar1=1.0 / 512, scalar2=None, op0=ALU.mult), [ak], ["ovlB0"])
            P.op("dve", lambda e: e.tensor_tensor(out=msq, in0=mean, in1=mean, op=ALU.mult), ["ovlB0"], ["ovlB0"])
            P.op("dve", lambda e: e.scalar_tensor_tensor(out=msq, in0=pmm[:, qb2, 0:16], scalar=1.0 / 512, in1=msq, op0=ALU.mult, op1=ALU.subtract), [qk2, "ovlB0"], ["ovlB0"])
            P.op("act", lambda e: e.activation(out=rstd, in_=msq, func=AF.Ln, bias=EPS_AP), ["ovlB0", "cols1"], ["ovlB1"])
            P.op("act", lambda e: e.activation(out=rstd, in_=rstd, func=AF.Exp, scale=-0.5), ["ovlB1"], ["ovlB1"])
            P.op("dve", lambda e: e.tensor_tensor(out=nmr, in0=mean, in1=rstd, op=ALU.mult), ["ovlB0", "ovlB1"], ["ovlB1"])
            for c in range(4):
                P.op("dve", lambda e, c=c: e.tensor_tensor(out=v4(TMP)[:, c, :], in0=v4(DWS)[:, c, :], in1=rstd, op=ALU.mult), ["s_dws", "ovlB1"], ["s_tmp"])
                P.op("dve", lambda e, c=c: e.tensor_tensor(out=v4(TMP)[:, c, :], in0=v4(TMP)[:, c, :], in1=nmr, op=ALU.subtract), ["s_tmp", "ovlB1"], ["s_tmp"])
                P.op("act", lambda e, c=c: e.activation(out=mixT[:, 4 + c, 512:528], in_=v4(TMP)[:, c, :], func=AF.Silu, scale=cols[:, 29 + c:30 + c], bias=cols[:, 33 + c:34 + c]),
                     ["s_tmp", "cols"], ["mixcv4"])

          def S4():
            wout_tile(4, 16, sWO, ["mixhg4", "mixcv4"])
          return [S1, S2, S3, S4]

        def sample_part2(sWO):
            for f in sample_part2_stages(sWO):
                f()

        def sample_barrier():
            P.op("dve", lambda e: e.memset(stat[:, 63:64], 0.0), [], SMK + ["ovl0a", "ovl0b", "ovl1", "statx"])

        next_parts, next_slots = None, None
        for seg in range(nseg):
            if seg == 0:
                slots = {"QF": load_win(0, seg), "IG": load_win(1, seg), "AB": load_win(2, seg)}
                parts = [tile_parts(seg, j, slots, last_tile=(seg == nseg - 1 and j == 3)) for j in range(4)]
            else:
                slots, parts = next_slots, next_parts
            sQF, sIG, sAB = slots["QF"], slots["IG"], slots["AB"]
            sWO = load_wout(seg)
            if seg == 0:
                build_diag()
            smp = with_sample and seg == 0
            smp_last = with_sample and seg == nseg - 1
            p2segs = list(range(0, nseg - 1))
            my_b = []
            if with_sample and seg in p2segs:
                cnt = [NSMP] if len(p2segs) == 1 else ([2] + [-(-(NSMP - 2) // (len(p2segs) - 1))] * (len(p2segs) - 1))
                k0 = sum(cnt[:seg])
                my_b = list(range(k0, min(NSMP, k0 + cnt[seg])))
            p1_b = list(range(NSMP)) if (with_sample and nseg == 1) else []
            if seg == 0:
                meta_prologue(sQF, sIG, sAB)
            if dbg_stop == "meta":
                break
            if smp:
                sample_part1(sQF, sIG, sAB)
            pcv = preconv_chunks() if (seg == 0 and nseg > 1 and dbg_stop is None) else []
            for f in pcv:
                f()
            pcv = []
            pm_state["banks"] = [2, 3, 4, 5]
            aux["eng"] = "dve" if seg == 0 else "pool"
            parts[0]["A0a"]()
            parts[0]["A0b"]()
            parts[1]["A0a"]()
            run_step(None, parts[0], parts[1], None, None, parts[2])
            for j in range(4):
                fl = None
                if j == 3:
                    conv_rest = conv_chunks()
                    if not p1_b:
                        fl, conv_rest = conv_rest[:4], conv_rest[4:]
                run_step(parts[j], parts[j + 1] if j < 3 else None, parts[j + 2] if j < 2 else None, parts[j - 1] if j > 0 else None,
                         fl, parts[j + 3] if j < 1 else None)
                if p1_b:
                    if j == 0:
                        sample_tok(SB1)
                    for b in range(4 * j, 4 * j + 4):
                        sample_b(SB1, b, b == 0, b == NSMP - 1)
                for f in pcv[:14]:
                    f()
                pcv = pcv[14:]
            parts[3]["B4b"]()
            if seg == 0 and preconv_done:
                P.op("dve", lambda e: e.memset(stat[:, 62:63], 0.0), [], ["bnc%d" % i for i in range(4)] + ["hn2T%d" % j for j in range(4)] + ["staty"])
            pm_state["banks"] = [0, 1, 2, 3, 4, 5]
            if p1_b:
                sample_barrier()
            seg_1bc(seg, sWO, seg == nseg - 1, conv_rest, sample_part2_stages(sWO) if smp_last else None)
            qslots = {}

            def ensure_q(qq):
                if qq <= 3 and qq not in qslots:
                    qslots[qq] = (load_wup(qq, seg), load_wdown(qq, seg))

            def cached(qq):
                return qq <= 3
            for q in range(4):
                ensure_q(q)
                if cached(q + 1):
                    ensure_q(q + 1)
                sU, sD = qslots[q]
                hooks = None
                if q == 3 and seg + 1 < nseg:
                    next_slots = {"QF": load_win(0, seg + 1), "IG": load_win(1, seg + 1), "AB": None}
                    next_parts = [tile_parts(seg + 1, j, next_slots, last_tile=(seg + 1 == nseg - 1 and j == 3)) for j in range(4)]
                    hooks = {0: next_parts[0]["A0a"], 1: next_parts[0]["A0b"], 3: next_parts[1]["A0a"]}

                    def _ab(ns=next_slots, sg=seg + 1):
                        ns["AB"] = load_win(2, sg)
                    hooks["after_up"] = _ab
                if my_b and q == 0:
                    sample_tok(SB2)
                    pipe = sample_pipe(SB2, my_b)
                mlp_quarter(seg, q, sU, sD, 4, NSMP if smp_last else 0, hooks, pipe if my_b else None)
                if my_b and q == 3:
                    while not pipe():
                        pass
        dma(shp.rearrange("h d v -> d h v"), S[:], ["S0", "S1", "S2", "S3"], [], "o_shp", is_out=True)

        with nc.Block() as block:
            @block.sync
            def _(e):
                P.replay("sp", e)

            @block.scalar
            def _(e):
                P.replay("act", e)

            @block.vector
            def _(e):
                P.replay("dve", e)

            @block.gpsimd
            def _(e):
                P.replay("pool", e)

            @block.tensor
            def _(e):
                P.replay("pe", e)
    return nc


def _consts():
    ident = np.eye(128, dtype=np.float32)
    s = np.arange(128)[:, None]
    t = np.arange(128)[None, :]
    mask2 = ((s // 64 == t // 64) & (s <= t)).astype(np.float32)
    mscan = np.ones((128, 128), np.float32)
    mscan[:, 0] = 0.0
    mscan[:, 64] = 0.0
    return ident, mask2, mscan


def make_in_maps(inputs, nseg=4):
    f = lambda a: np.ascontiguousarray(np.asarray(a, dtype=np.float32))
    ident, mask2, mscan = _consts()
    rows = np.concatenate([
        f(inputs["hg_lb"]).reshape(8, 128), f(inputs["norm1_g"]).reshape(8, 128), f(inputs["norm2_g"]).reshape(8, 128),
        f(inputs["hg_onorm_g"]).reshape(1, 128), f(inputs["conv_b"]).reshape(4, 128), f(inputs["conv_ln_g"]).reshape(4, 128),
        f(inputs["conv_ln_b"]).reshape(4, 128)], axis=0)
    shared = {
        "meta": f(inputs["meta_tokens"]), "w_in": f(inputs["w_in"][0]), "w_out": f(inputs["w_out"][0]),
        "w_up": f(inputs["w_up"][0]), "w_down": f(inputs["w_down"][0]), "rows": f(rows), "conv_w": f(inputs["conv_w"][0]),
        "final_g": f(inputs["final_g"]).reshape(1, 1024), "ident": ident, "mask2": mask2, "mscan": mscan,
    }
    xp = f(inputs["x_prompt"])
    xs = f(inputs["x_sample"])
    sh = f(inputs["state_hgrn"])
    sc = f(inputs["state_conv"])
    maps = []
    for c in range(xp.shape[0]):
        m = dict(shared)
        m["xp"] = xp[c, :nseg * 512]
        m["xs"] = xs[c * NSMP:(c + 1) * NSMP, 0]
        m["sh"] = sh[0, c * NSMP:(c + 1) * NSMP]
        m["sc"] = sc[0, c * NSMP:(c + 1) * NSMP]
        maps.append(m)
    return maps


_NC_CACHE = {}


def kernel(**inputs):
    nseg = SEQ // 512
    if nseg not in _NC_CACHE:
        _NC_CACHE[nseg] = build_program(nseg)
    nc = _NC_CACHE[nseg]
    maps = make_in_maps(inputs, nseg)
    res = run_bass_kernel_spmd(nc, maps, core_ids=list(range(NCORES)))
    r = res.results
    y_prompt = np.stack([r[c]["yp"] for c in range(NCORES)], axis=0)
    y_sample = np.concatenate([r[c]["ys"] for c in range(NCORES)], axis=0)[:, None, :]
    shp = np.stack([r[c]["shp"] for c in range(NCORES)], axis=0)[None]
    scp = np.stack([r[c]["scp"] for c in range(NCORES)], axis=0)[None]
    shs = np.concatenate([r[c]["shs"] for c in range(NCORES)], axis=0)[None]
    scs = np.concatenate([r[c]["scs"] for c in range(NCORES)], axis=0)[None]
    return (y_prompt.astype(np.float32), y_sample.astype(np.float32), shp.astype(np.float32), scp.astype(np.float32),
            shs.astype(np.float32), scs.astype(np.float32))
```
